# Optimizing a Trainium2 kernel written in Bass

```python
import math
import jax
import jax.numpy as jnp
from jax import lax
import numpy as np

D_MODEL = 2048
BATCH = 2
SEQ = 16384
DEPTH = 1

N_MEM = 256
MIX_WIDTH = D_MODEL
GLA_WIDTH = MIX_WIDTH // 2
DIFF_WIDTH = MIX_WIDTH // 4
MEM_WIDTH = MIX_WIDTH // 4

GLA_HEADS = 4
GLA_DV = GLA_WIDTH // GLA_HEADS
GLA_DK = GLA_DV // 2
GLA_LOWRANK = 16
GLA_TAU = 16.0
GLA_CHUNK = 64
GLA_NORM_EPS = 1e-6

DIFF_HEADS = 4
DIFF_DV = DIFF_WIDTH // DIFF_HEADS
DIFF_DQK = DIFF_DV // 2
DIFF_NORM_EPS = 1e-5
Q_BLOCK = 128

MEM_HEADS = 4
MEM_DH = MEM_WIDTH // MEM_HEADS

ROPE_THETA = 500000.0
ROT_DIM = DIFF_DQK // 4
LN_EPS = 1e-5
DEEPNORM_ALPHA = (2.0 * DEPTH) ** 0.25
DEEPNORM_BETA = (8.0 * DEPTH) ** -0.25

IN_SIZES = (
    GLA_HEADS * GLA_DK,
    GLA_HEADS * GLA_DK,
    GLA_WIDTH,
    GLA_WIDTH,
    GLA_LOWRANK,
    DIFF_WIDTH,
    DIFF_WIDTH,
    DIFF_WIDTH,
    DIFF_WIDTH,
    MEM_WIDTH,
    MEM_WIDTH,
)
IN_WIDTH = int(sum(IN_SIZES))
IN_SPLITS = tuple(int(s) for s in np.cumsum(IN_SIZES)[:-1])

kernel_name = 'hybrid_gla_diffattn_memxattn_deepnorm'


def _lambda_init(layer):
    return 0.8 - 0.6 * math.exp(-0.3 * layer)


def _layernorm(x, g, b):
    xf = x.astype(jnp.float32)
    mu = jnp.mean(xf, axis=-1, keepdims=True)
    var = jnp.mean(jnp.square(xf - mu), axis=-1, keepdims=True)
    y = (xf - mu) * lax.rsqrt(var + LN_EPS) * g.astype(jnp.float32) + b.astype(jnp.float32)
    return y.astype(x.dtype)


def _rmsnorm(x, g, eps):
    xf = x.astype(jnp.float32)
    y = xf * lax.rsqrt(jnp.mean(xf * xf, axis=-1, keepdims=True) + eps) * g.astype(jnp.float32)
    return y.astype(x.dtype)


def _rotary_tables(positions):
    half = jnp.arange(0, ROT_DIM, 2, dtype=jnp.float32) / ROT_DIM
    inv_freq = jnp.power(jnp.float32(ROPE_THETA), -half)
    ang = positions.astype(jnp.float32)[..., None] * inv_freq
    return jnp.cos(ang)[:, :, None, :], jnp.sin(ang)[:, :, None, :]


def _partial_rotary(x, cos, sin):
    half = ROT_DIM // 2
    x1 = x[..., :half].astype(jnp.float32)
    x2 = x[..., half:ROT_DIM].astype(jnp.float32)
    r1 = (x1 * cos - x2 * sin).astype(x.dtype)
    r2 = (x2 * cos + x1 * sin).astype(x.dtype)
    return jnp.concatenate([r1, r2, x[..., ROT_DIM:]], axis=-1)


def _gla_chunked(q, k, v, log_a):
    B, S, H, Dk = q.shape
    Dv = v.shape[-1]
    C = GLA_CHUNK
    nc = S // C

    def chunked(t):
        return t.astype(jnp.float32).reshape(B, nc, C, H, t.shape[-1]).transpose(1, 0, 3, 2, 4)

    qc = chunked(q * (GLA_DK ** -0.5))
    kc = chunked(k)
    vc = chunked(v)
    bc = lax.cumsum(chunked(log_a), axis=3)
    causal = jnp.tril(jnp.ones((C, C), dtype=bool))

    def step(state, inp):
        qi, ki, vi, bi = inp
        o_inter = jnp.einsum('bhck,bhkv->bhcv', qi * jnp.exp(bi), state)
        rel = bi[:, :, :, None, :] - bi[:, :, None, :, :]
        decay = jnp.exp(jnp.where(causal[:, :, None], rel, -jnp.inf))
        scores = jnp.einsum('bhik,bhjk,bhijk->bhij', qi, ki, decay)
        o_intra = jnp.einsum('bhij,bhjv->bhiv', scores, vi)
        b_last = bi[:, :, -1:, :]
        k_dec = ki * jnp.exp(b_last - bi)
        new_state = state * jnp.exp(b_last[:, :, 0, :])[..., None] + jnp.einsum('bhck,bhcv->bhkv', k_dec, vi)
        return new_state, o_inter + o_intra

    state0 = jnp.zeros((B, H, Dk, Dv), jnp.float32)
    _, o = lax.scan(step, state0, (qc, kc, vc, bc))
    return o.transpose(1, 0, 3, 2, 4).reshape(B, S, H, Dv).astype(v.dtype)


def _diff_attention(q, k, v, lam):
    B, S, H, _, Dqk = q.shape
    Dv = v.shape[-1]
    nb = S // Q_BLOCK
    qb = (q * (Dqk ** -0.5)).reshape(B, nb, Q_BLOCK, H, 2, Dqk).transpose(1, 0, 2, 3, 4, 5)
    key_pos = jnp.arange(S)

    def block(args):
        qi, i = args
        s = jnp.einsum('bqhcd,bkhcd->bhcqk', qi, k, preferred_element_type=jnp.float32)
        q_pos = i * Q_BLOCK + jnp.arange(Q_BLOCK)
        mask = key_pos[None, :] <= q_pos[:, None]
        p = jax.nn.softmax(jnp.where(mask, s, -jnp.inf), axis=-1)
        w = p[:, :, 0] - lam * p[:, :, 1]
        return jnp.einsum('bhqk,bkhd->bqhd', w.astype(v.dtype), v)

    out = lax.map(block, (qb, jnp.arange(nb)))
    return out.transpose(1, 0, 2, 3, 4).reshape(B, S, H, Dv)


def _memory_attention(q, mk, mv):
    s = jnp.einsum('bshd,bmhd->bhsm', q * (MEM_DH ** -0.5), mk, preferred_element_type=jnp.float32)
    p = jax.nn.softmax(s, axis=-1)
    return jnp.einsum('bhsm,bmhd->bshd', p.astype(mv.dtype), mv)


def setup_inputs(seed: int = 0) -> dict:
    key = jax.random.key(seed)
    ks = jax.random.split(key, 16)
    f32 = jnp.float32
    x = jax.random.normal(ks[0], (BATCH, SEQ, D_MODEL), f32)
    mem = jax.random.normal(ks[1], (BATCH, N_MEM, D_MODEL), f32)
    offset = jax.random.randint(ks[2], (BATCH, 1), 0, 4096, dtype=jnp.int32)
    positions = offset + jnp.arange(SEQ, dtype=jnp.int32)[None, :]
    w_in = jax.random.normal(ks[3], (DEPTH, D_MODEL, IN_WIDTH), f32) * D_MODEL ** -0.5
    w_gk_up = jax.random.normal(ks[4], (DEPTH, GLA_LOWRANK, GLA_HEADS * GLA_DK), f32) * GLA_LOWRANK ** -0.5
    b_gk_up = 0.1 * jax.random.normal(ks[5], (DEPTH, GLA_HEADS * GLA_DK), f32)
    gla_norm_g = 1.0 + 0.02 * jax.random.normal(ks[6], (DEPTH, GLA_DV), f32)
    lambda_q1 = 0.1 * jax.random.normal(ks[7], (DEPTH, DIFF_DQK), f32)
    lambda_k1 = 0.1 * jax.random.normal(ks[8], (DEPTH, DIFF_DQK), f32)
    lambda_q2 = 0.1 * jax.random.normal(ks[9], (DEPTH, DIFF_DQK), f32)
    lambda_k2 = 0.1 * jax.random.normal(ks[10], (DEPTH, DIFF_DQK), f32)
    diff_norm_g = 1.0 + 0.02 * jax.random.normal(ks[11], (DEPTH, DIFF_DV), f32)
    w_mem_kv = jax.random.normal(ks[12], (DEPTH, D_MODEL, 2 * MEM_WIDTH), f32) * D_MODEL ** -0.5
    w_out = jax.random.normal(ks[13], (DEPTH, MIX_WIDTH, D_MODEL), f32) * (MIX_WIDTH ** -0.5 * DEEPNORM_BETA)
    ln_g = 1.0 + 0.02 * jax.random.normal(ks[14], (DEPTH, D_MODEL), f32)
    ln_b = 0.02 * jax.random.normal(ks[15], (DEPTH, D_MODEL), f32)
    return {'x': x, 'mem': mem, 'positions': positions, 'w_in': w_in,
            'w_gk_up': w_gk_up, 'b_gk_up': b_gk_up, 'gla_norm_g': gla_norm_g,
            'lambda_q1': lambda_q1, 'lambda_k1': lambda_k1,
            'lambda_q2': lambda_q2, 'lambda_k2': lambda_k2,
            'diff_norm_g': diff_norm_g, 'w_mem_kv': w_mem_kv, 'w_out': w_out,
            'ln_g': ln_g, 'ln_b': ln_b}


def reference(x, mem, positions, w_in, w_gk_up, b_gk_up, gla_norm_g,
              lambda_q1, lambda_k1, lambda_q2, lambda_k2, diff_norm_g,
              w_mem_kv, w_out, ln_g, ln_b):
    B, S, _ = x.shape
    M = mem.shape[1]
    f32 = jnp.float32
    cos, sin = _rotary_tables(positions)
    h = x
    for l in range(DEPTH):
        proj = jnp.einsum('bsd,de->bse', h, w_in[l])
        (g_q, g_k, g_v, g_g, g_lr, d_q, d_k, d_v, d_g, m_q, m_g) = jnp.split(proj, IN_SPLITS, axis=-1)

        gk_logit = (jnp.einsum('bsr,rk->bsk', g_lr, w_gk_up[l]) + b_gk_up[l]).astype(f32)
        log_a = jax.nn.log_sigmoid(gk_logit) / GLA_TAU
        gla = _gla_chunked(g_q.reshape(B, S, GLA_HEADS, GLA_DK),
                           g_k.reshape(B, S, GLA_HEADS, GLA_DK),
                           g_v.reshape(B, S, GLA_HEADS, GLA_DV),
                           log_a.reshape(B, S, GLA_HEADS, GLA_DK))
        gla = _rmsnorm(gla, gla_norm_g[l], GLA_NORM_EPS).reshape(B, S, GLA_WIDTH) * jax.nn.silu(g_g)

        dq = _partial_rotary(d_q.reshape(B, S, 2 * DIFF_HEADS, DIFF_DQK), cos, sin).reshape(B, S, DIFF_HEADS, 2, DIFF_DQK)
        dk = _partial_rotary(d_k.reshape(B, S, 2 * DIFF_HEADS, DIFF_DQK), cos, sin).reshape(B, S, DIFF_HEADS, 2, DIFF_DQK)
        lam_init = _lambda_init(l)
        lam = (jnp.exp(jnp.sum(lambda_q1[l].astype(f32) * lambda_k1[l].astype(f32)))
               - jnp.exp(jnp.sum(lambda_q2[l].astype(f32) * lambda_k2[l].astype(f32))) + lam_init)
        diff = _diff_attention(dq, dk, d_v.reshape(B, S, DIFF_HEADS, DIFF_DV), lam)
        diff = (_rmsnorm(diff, diff_norm_g[l], DIFF_NORM_EPS) * (1.0 - lam_init)).reshape(B, S, DIFF_WIDTH) * jax.nn.silu(d_g)

        mkv = jnp.einsum('bmd,de->bme', mem, w_mem_kv[l])
        m_k, m_v = jnp.split(mkv, 2, axis=-1)
        xat = _memory_attention(m_q.reshape(B, S, MEM_HEADS, MEM_DH),
                                m_k.reshape(B, M, MEM_HEADS, MEM_DH),
                                m_v.reshape(B, M, MEM_HEADS, MEM_DH))
        xat = xat.reshape(B, S, MEM_WIDTH) * jax.nn.silu(m_g)

        mix = jnp.concatenate([gla, diff, xat], axis=-1)
        out = jnp.einsum('bse,ed->bsd', mix, w_out[l])
        h = _layernorm(DEEPNORM_ALPHA * h + out, ln_g[l], ln_b[l])
    return h
```

```python
import math
from contextlib import ExitStack

import numpy as np
import concourse.bass as bass
import concourse.mybir as mybir
from concourse.bass_utils import run_bass_kernel_spmd

F32 = mybir.dt.float32
BF16 = mybir.dt.bfloat16
I32 = mybir.dt.int32
AF = mybir.ActivationFunctionType
ALU = mybir.AluOpType

D = 2048
S = 16384
NT = S // 512
NKC = D // 128
SQ = S // 4
WF_N = 7 * 128 + 16
WT_N = 1024
LAM_INIT = 0.8 - 0.6 * math.exp(-0.3 * 0)
TWO_PI = 2.0 * math.pi
C1 = 6.28125
C2 = TWO_PI - C1

FB = {"gq": 0, "gk": 128, "dq": 256, "dk": 384, "dqs": 512, "dks": 640, "mq": 768, "lr": 896}

CI_INVF, CI_SGN, CI_NOTROT, CI_HALFPI, CI_ONE, CI_EPSG, CI_EPSD, CI_EPSL = range(8)


class Op:
    __slots__ = ("eng", "fn", "waits", "signal", "token", "inc", "sigidx")

    def __init__(self, eng, fn):
        self.eng = eng
        self.fn = fn
        self.waits = []
        self.signal = False
        self.token = None
        self.inc = 1
        self.sigidx = None


class Sched:
    ENGS = ["pe", "act", "dve", "pool", "sp"]
    NSLOT = 8

    def __init__(self):
        self.ops = {e: [] for e in self.ENGS}
        self.last_writer = {}
        self.readers = {}
        self.slot_prev = {}
        self.slot_uses = {}
        self.next_slot = {"sp": 0, "pool": 0}
        self.dma_sems = {}
        self.eng_sems = {}
        self.cc_sem = None

    @staticmethod
    def _is_psum(r):
        return isinstance(r, tuple) and r[0] in ("PG", "PS", "PAB")

    def _deps(self, op, reads, writes):
        writes = list(writes) + [r for r in reads if self._is_psum(r)]
        reads = [r for r in reads if not self._is_psum(r)]
        deps = []
        for r in reads:
            w = self.last_writer.get(r)
            if w is not None:
                deps.append(w)
        for w_ in writes:
            w = self.last_writer.get(w_)
            if w is not None:
                deps.append(w)
            deps.extend(self.readers.get(w_, ()))
        for w_ in writes:
            self.last_writer[w_] = op
            self.readers[w_] = []
        for r in reads:
            lst = self.readers.setdefault(r, [])
            if op.token is None:
                lst[:] = [o for o in lst if not (o.eng == op.eng and o.token is None)]
            lst.append(op)
        seen = set()
        for d in deps:
            if d is op or id(d) in seen:
                continue
            seen.add(id(d))
            if d.eng == "pe" and op.eng == "pe" and d.token is None and op.token is None:
                continue
            if d.token is None:
                d.signal = True
            op.waits.append(d)

    stage = 0
    level = 99

    def op(self, eng, fn, reads=(), writes=()):
        o = Op(eng, fn)
        if self.stage > self.level:
            return o
        self._deps(o, reads, writes)
        self.ops[eng].append(o)
        return o

    def dma(self, q, fn, reads=(), writes=()):
        o = Op(q, fn)
        if self.stage > self.level:
            return o
        slot = self.next_slot[q] % self.NSLOT
        self.next_slot[q] += 1
        key = (q, slot)
        uses = self.slot_uses.get(key, 0) + 1
        self.slot_uses[key] = uses
        o.token = (("dma", q, slot), 16 * uses)
        o.inc = 16
        prev = self.slot_prev.get(key)
        if prev is not None:
            o.waits.append(prev)
        self.slot_prev[key] = o
        self._deps(o, reads, writes)
        self.ops[q].append(o)
        return o

    def collective(self, fn, reads=(), writes=()):
        o = Op("pool", fn)
        if self.stage > self.level:
            return o
        o.token = (("cc",), 1)
        o.inc = 1
        self._deps(o, reads, writes)
        self.ops["pool"].append(o)
        return o

    def fence(self, eng, deps):
        o = Op(eng, None)
        for d in deps:
            if d.token is None and d.sigidx is None and not any(d is x for x in self.ops[d.eng]):
                continue
            if d.token is None:
                d.signal = True
            o.waits.append(d)
        self.ops[eng].append(o)
        return o

    def emit(self, nc, es):
        for e in self.ENGS:
            self.eng_sems[e] = es.enter_context(nc.semaphore("sem_" + e))
        for q in ("sp", "pool"):
            for s in range(self.NSLOT):
                self.dma_sems[("dma", q, s)] = es.enter_context(nc.semaphore("dsem_%s_%d" % (q, s)))
        self.cc_sem = es.enter_context(nc.semaphore("ccsem"))
        for e in self.ENGS:
            n = 0
            for o in self.ops[e]:
                if o.token is None and o.signal:
                    n += 1
                    o.sigidx = n
        block = es.enter_context(nc.Block())

        def tok(o):
            if o.token is None:
                return ("eng", o.eng), o.sigidx
            return o.token

        def semof(k):
            if k[0] == "eng":
                return self.eng_sems[k[1]]
            if k[0] == "cc":
                return self.cc_sem
            return self.dma_sems[k]

        def run(e):
            def body(eng):
                known = {}
                for o in self.ops[e]:
                    for d in o.waits:
                        k, v = tok(d)
                        if known.get(k, 0) < v:
                            eng.wait_ge(semof(k), v)
                            known[k] = v
                    if o.fn is None:
                        continue
                    ins = o.fn(eng)
                    if o.token is not None:
                        if o.token[0][0] == "cc":
                            ins.then_inc(self.cc_sem)
                        else:
                            ins.then_inc(self.dma_sems[o.token[0]], 16)
                    elif o.signal:
                        ins.then_inc(self.eng_sems[e], 1)
            return body

        block.tensor(run("pe"))
        block.scalar(run("act"))
        block.vector(run("dve"))
        block.gpsimd(run("pool"))
        block.sync(run("sp"))


def build_nc(level=99):
    nc = bass.Bass("TRN2", target_bir_lowering=False)
    es = ExitStack()
    sc = Sched()
    sc.level = level

    def dram_in(name, shape, dt=F32):
        return nc.dram_tensor(name, list(shape), dt, kind="ExternalInput").ap()

    xT = dram_in("xT", [D, S])
    xq = dram_in("xq", [SQ, D])
    wF = dram_in("wF", [D, WF_N])
    wT = dram_in("wT", [D, WT_N])
    memT = dram_in("memT", [D, 256])
    wm = dram_in("wm", [D, 256])
    wup = dram_in("wup", [33, 128])
    wout = dram_in("wout", [D, D])
    posr = dram_in("posr", [128, S], I32)
    glag_d = dram_in("glag", [128, 256])
    dgg_d = dram_in("dgg", [128, 128])
    lamp_d = dram_in("lamp", [128, 256])
    lng_d = dram_in("lng", [128, D])
    lnb_d = dram_in("lnb", [128, D])
    cst_d = dram_in("cst", [128, 8])
    tri_d = dram_in("tri", [128, 128])
    tinc_d = dram_in("tinc", [128, 128])
    texc_d = dram_in("texc", [128, 128])
    ident_d = dram_in("ident", [128, 128])
    y = nc.dram_tensor("y", [SQ, D], F32, kind="ExternalOutput").ap()
    mixbuf = nc.dram_tensor("mixbuf", [S, 512], BF16)
    gath = nc.dram_tensor("gath", [8 * S, 512], BF16)

    def sb(name, shape, dt):
        return es.enter_context(nc.sbuf_tensor("s_" + name, list(shape), dt))

    arena = sb("arena", [128, 16384 + 128 * 129], BF16)
    KT = arena[:, 0:S]
    V1 = arena[:, 16384:16384 + (S // 128) * 129].rearrange("p (k c) -> p k c", c=129)
    WOUT = arena[:, 0:32768].rearrange("p (k n) -> p k n", n=2048)
    arena2 = sb("arena2", [128, NKC * (WF_N + WT_N)], BF16)
    WF = arena2[:, 0:NKC * WF_N].rearrange("p (k n) -> p k n", n=WF_N)
    WT = arena2[:, NKC * WF_N:NKC * (WF_N + WT_N)].rearrange("p (k n) -> p k n", n=WT_N)
    xb = sb("xb", [128, NKC, 512], BF16)
    memTb = xb[:, :, 0:256]
    wmb = xb[:, :, 256:512]
    mkT = sb("mkT", [128, 256], BF16)
    mv1 = sb("mv1", [128, 2, 129], BF16)
    wup_s = sb("wup_s", [33, 128], F32)
    cst = sb("cst", [128, 8], F32)
    tri = sb("tri", [128, 128], BF16)
    tinc = sb("tinc", [128, 128], F32)
    texc = sb("texc", [128, 128], F32)
    ident = sb("ident", [128, 128], F32)
    glag = sb("glag_s", [128, 256], F32)
    dgg = sb("dgg_s", [128, 128], F32)
    lamp = sb("lamp_s", [128, 256], F32)
    lam = sb("lam", [128, 1], F32)
    lamt = sb("lamt", [128, 4], F32)
    Sst = sb("Sst", [128, 256], F32)
    Sbf = sb("Sbf", [128, 256], BF16)
    posi = sb("posi", [128, 512], I32)
    bufA = sb("bufA", [128, 512], F32)
    bufK = sb("bufK", [128, 512], I32)
    bufR = sb("bufR", [128, 512], F32)
    bufM = sb("bufM", [128, 512], F32)
    sinS = sb("sinS", [128, 512], F32)
    cosT = sb("cosT", [128, 512], F32)
    T1 = sb("T1", [128, 512], F32)
    T2 = sb("T2", [128, 512], F32)
    qT = sb("qT", [128, 512], BF16)
    gqt = sb("gqt", [128, 512], BF16)
    gkt = sb("gkt", [128, 512], BF16)
    mqT = sb("mqT", [128, 512], BF16)
    lrT = sb("lrT", [33, 512], F32)
    SP = sb("SP", [128, 512], F32)
    EB = sb("EB", [128, 512], F32)
    ENB = sb("ENB", [128, 512], F32)
    ED = sb("ED", [128, 512], F32)
    gktok = sb("gktok", [128, 4, 128], F32)
    kdec = sb("kdec", [128, 4, 128], BF16)
    gvb = sb("gvb", [128, 4, 256], BF16)
    gsil = sb("gsil", [128, 4, 256], F32)
    dgs = sb("dgs", [128, 4, 128], F32)
    mgs = sb("mgs", [128, 4, 128], F32)
    sTm = sb("sTm", [128, 128], BF16)
    junk = sb("junk", [128, 256], F32)
    sml = sb("sml", [128, 16], F32)
    pm = sb("pm", [128, 2, 512], BF16)
    NPT = 4
    pT = [sb("pT%d" % i, [128, 512], BF16) for i in range(NPT)]
    tmpd = sb("tmpd", [128, 128], F32)
    dif = sb("dif", [128, 128], F32)
    mix = sb("mix", [128, 4, 512], BF16)

    off = [0]

    def carve(nelem_bf16):
        a = arena2[:, off[0]:off[0] + nelem_bf16]
        off[0] += nelem_bf16
        return a

    mrow = [carve(2 * 4 * 512).bitcast(F32).rearrange("p (h c) -> p h c", c=512) for _ in range(2)]
    mixT = [carve(16 * 128).rearrange("p (k c) -> p k c", c=128) for _ in range(2)]
    xrow = [carve(2 * D).bitcast(F32) for _ in range(2)]
    yrow = xrow
    lng = carve(2 * D).bitcast(F32)
    lnb = carve(2 * D).bitcast(F32)
    bnst = carve(2 * 32).bitcast(F32)
    assert off[0] <= NKC * (WF_N + WT_N)

    PB = [es.enter_context(nc.psum_tensor("pb%d" % i, [128, 512], F32)) for i in range(8)]
    PG = PB[0:3]
    PS = PB[3:5]
    PA = PB[5:8]
    gctr = [0]

    def gbank():
        i = gctr[0] % 3
        gctr[0] += 1
        return PG[i], ("PG", i)

    def pa_region(c, qb):
        idx = c * 4 + qb
        return PA[idx // 3][:, (idx % 3) * 144:(idx % 3) * 144 + 129], ("PAB", idx // 3)

    def col(i):
        return cst[:, i:i + 1]

    def ld_sp(dst, src, res):
        return sc.dma("sp", lambda e, d=dst, s=src: e.dma_start(out=d, in_=s), writes=[res])

    def ld_cast(dst, src, res):
        return sc.dma("pool", lambda e, d=dst, s=src: e.dma_start(out=d, in_=s), writes=[res])

    ld_sp(cst[:], cst_d, "cst")
    ld_sp(tinc[:], tinc_d, "tinc")
    ld_sp(texc[:], texc_d, "texc")
    ld_sp(wup_s[:], wup, "wup")
    ld_sp(glag[:], glag_d, "glag")
    ld_sp(dgg[:], dgg_d, "dgg")
    ld_sp(lamp[:], lamp_d, "lamp")
    ld_cast(tri[:], tri_d, "tri")
    ld_sp(ident[:], ident_d, "ident")
    XB0 = [("xb", j) for j in range(4)]
    sc.dma("pool", lambda e: e.dma_start(out=memTb, in_=memT.rearrange("(k p) m -> p k m", p=128)), writes=XB0)
    sc.dma("pool", lambda e: e.dma_start(out=wmb, in_=wm.rearrange("(k p) m -> p k m", p=128)), writes=XB0)
    wTv = wT.rearrange("(k p) n -> p k n", p=128)
    wFv = wF.rearrange("(k p) n -> p k n", p=128)
    for j in range(4):
        ld_cast(WT[:, 4 * j:4 * j + 4, :], wTv[:, 4 * j:4 * j + 4, :], ("WT", j))
    for j in range(4):
        ld_cast(WF[:, 4 * j:4 * j + 4, :], wFv[:, 4 * j:4 * j + 4, :], ("WF", j))
    WTres = [("WT", j) for j in range(4)]
    WFres = [("WF", j) for j in range(4)]

    sc.op("pool", lambda e: e.tensor_scalar(out=dgg[:], in0=dgg[:], scalar1=float(1.0 - LAM_INIT), scalar2=None, op0=ALU.mult),
          reads=["dgg"], writes=["dgg"])
    sc.op("pool", lambda e: e.memset(Sst[:], 0.0), writes=["S"])
    sc.op("pool", lambda e: e.memset(Sbf[:], 0.0), writes=["Sbf"])
    sc.op("pool", lambda e: e.memset(lrT[:], 0.0), writes=["lrT"])
    sc.op("pool", lambda e: e.memset(lrT[32:33, :], 1.0), writes=["lrT"])
    sc.op("pool", lambda e: e.memset(V1[:, :, 128:129], 1.0), writes=["V1ones"])
    sc.op("pool", lambda e: e.memset(mv1[:, :, 128:129], 1.0), writes=["mv1ones"])

    sc.stage = 1
    sc.op("dve", lambda e: e.tensor_tensor(out=junk[:, 0:64], in0=lamp[:, 0:64], in1=lamp[:, 64:128], op=ALU.mult),
          reads=["lamp"], writes=["junk"])
    sc.op("dve", lambda e: e.tensor_reduce(out=lamt[:, 0:1], in_=junk[:, 0:64], axis=mybir.AxisListType.X, op=ALU.add),
          reads=["junk"], writes=["lamt0"])
    sc.op("dve", lambda e: e.tensor_tensor(out=junk[:, 64:128], in0=lamp[:, 128:192], in1=lamp[:, 192:256], op=ALU.mult),
          reads=["lamp"], writes=["junk2"])
    sc.op("dve", lambda e: e.tensor_reduce(out=lamt[:, 1:2], in_=junk[:, 64:128], axis=mybir.AxisListType.X, op=ALU.add),
          reads=["junk2"], writes=["lamt1"])
    sc.op("act", lambda e: e.activation(out=lamt[:, 2:4], in_=lamt[:, 0:2], func=AF.Exp),
          reads=["lamt0", "lamt1"], writes=["lamt2"])
    sc.op("dve", lambda e: e.tensor_tensor(out=lam[:], in0=lamt[:, 2:3], in1=lamt[:, 3:4], op=ALU.subtract),
          reads=["lamt2"], writes=["lam"])
    sc.op("dve", lambda e: e.tensor_scalar(out=lam[:], in0=lam[:], scalar1=float(LAM_INIT), scalar2=None, op0=ALU.add),
          reads=["lam"], writes=["lam"])

    sc.stage = 2
    pk, rk = gbank()
    def mk_mm(e):
        ins = None
        for kc in range(NKC):
            ins = e.matmul(pk[:, 0:256], lhsT=wmb[:, kc, 0:128], rhs=memTb[:, kc, :], start=(kc == 0), stop=(kc == NKC - 1))
        return ins
    sc.op("pe", mk_mm, reads=XB0, writes=[rk])
    sc.op("act", lambda e: e.activation(out=mkT[:], in_=pk[:, 0:256], func=AF.Copy), reads=[rk], writes=["mkT"])
    pv_, rv_ = gbank()
    def mv_mm(e):
        ins = None
        for mb in range(2):
            for kc in range(NKC):
                ins = e.matmul(pv_[:, mb * 128:(mb + 1) * 128], lhsT=memTb[:, kc, mb * 128:(mb + 1) * 128],
                               rhs=wmb[:, kc, 128:256], start=(kc == 0), stop=(kc == NKC - 1))
        return ins
    sc.op("pe", mv_mm, reads=XB0, writes=[rv_])
    sc.op("act", lambda e: e.activation(out=mv1[:, :, 0:128], in_=pv_[:, 0:256].rearrange("p (m d) -> p m d", d=128), func=AF.Copy),
          reads=[rv_], writes=["mv1"])

    xTv = xT.rearrange("(k p) s -> p k s", p=128)
    mixv = mixbuf.ap().rearrange("(n p) c -> p n c", p=128)
    mix_stores = []
    ptc = [0]
    psc = [0]

    def issue_x(t):
        T0 = t * 512
        for j in range(4):
            sc.dma("pool", lambda e, j=j, T0=T0: e.dma_start(out=xb[:, 4 * j:4 * j + 4, :], in_=xTv[:, 4 * j:4 * j + 4, T0:T0 + 512]),
                   writes=[("xb", j)])

    def issue_pos(t):
        T0 = t * 512
        sc.dma("sp", lambda e, T0=T0: e.dma_start(out=posi[:], in_=posr[:, T0:T0 + 512]), writes=["posi"])

    for t in range(NT):
        T0 = t * 512
        sc.stage = 0
        if t == 0:
            issue_x(0)
            issue_pos(0)
        XB = [("xb", j) for j in range(4)]

        sc.stage = 3
        sc.op("pool", lambda e: e.tensor_copy(out=bufA[:], in_=posi[:]), reads=["posi"], writes=["bufA"])
        sc.op("pool", lambda e: e.tensor_scalar(out=bufA[:], in0=bufA[:], scalar1=col(CI_INVF), scalar2=None, op0=ALU.mult),
              reads=["bufA", "cst"], writes=["bufA"])
        sc.op("pool", lambda e: e.tensor_scalar(out=bufK[:], in0=bufA[:], scalar1=float(1.0 / TWO_PI), scalar2=None, op0=ALU.mult),
              reads=["bufA"], writes=["bufK"])
        sc.op("dve", lambda e: e.scalar_tensor_tensor(out=bufR[:], in0=bufK[:], scalar=float(-C1), in1=bufA[:], op0=ALU.mult, op1=ALU.add),
              reads=["bufK", "bufA"], writes=["bufR"])
        sc.op("dve", lambda e: e.scalar_tensor_tensor(out=bufR[:], in0=bufK[:], scalar=float(-C2), in1=bufR[:], op0=ALU.mult, op1=ALU.add),
              reads=["bufK", "bufR"], writes=["bufR"])
        sc.op("pool", lambda e: e.tensor_scalar(out=bufM[:], in0=bufR[:], scalar1=float(math.pi), scalar2=float(-TWO_PI), op0=ALU.is_gt, op1=ALU.mult),
              reads=["bufR"], writes=["bufM"])
        sc.op("pool", lambda e: e.tensor_tensor(out=sinS[:], in0=bufM[:], in1=bufR[:], op=ALU.add),
              reads=["bufM", "bufR"], writes=["sinS"])
        sc.op("pool", lambda e: e.tensor_scalar(out=bufM[:], in0=bufR[:], scalar1=float(math.pi / 2), scalar2=float(-TWO_PI), op0=ALU.is_gt, op1=ALU.mult),
              reads=["bufR"], writes=["bufM"])
        sc.op("pool", lambda e: e.tensor_tensor(out=cosT[:], in0=bufM[:], in1=bufR[:], op=ALU.add),
              reads=["bufM", "bufR"], writes=["cosT"])
        sc.op("act", lambda e: e.activation(out=sinS[:], in_=sinS[:], func=AF.Sin), reads=["sinS"], writes=["sinS"])
        sc.op("act", lambda e: e.activation(out=cosT[:], in_=cosT[:], func=AF.Sin, bias=col(CI_HALFPI)), reads=["cosT", "cst"], writes=["cosT"])
        sc.op("pool", lambda e: e.tensor_scalar(out=sinS[:], in0=sinS[:], scalar1=col(CI_SGN), scalar2=None, op0=ALU.mult),
              reads=["sinS", "cst"], writes=["sinS"])
        sc.op("pool", lambda e: e.tensor_scalar(out=cosT[:], in0=cosT[:], scalar1=col(CI_NOTROT), scalar2=None, op0=ALU.max),
              reads=["cosT", "cst"], writes=["cosT"])

        sc.stage = 4
        for tb in range(4):
            pa_, ra_ = gbank()
            pb_, rb_ = gbank()
            def tproj(e, tb=tb, pa_=pa_, pb_=pb_):
                ins = None
                for kc in range(NKC):
                    lt = xb[:, kc, tb * 128:(tb + 1) * 128]
                    e.matmul(pa_[:, :], lhsT=lt, rhs=WT[:, kc, 0:512], start=(kc == 0), stop=(kc == NKC - 1))
                    ins = e.matmul(pb_[:, :], lhsT=lt, rhs=WT[:, kc, 512:1024], start=(kc == 0), stop=(kc == NKC - 1))
                return ins
            sc.stage = 4
            sc.op("pe", tproj, reads=XB + WTres, writes=[ra_, rb_])
            sc.stage = 4.1
            sc.op("act", lambda e, tb=tb, pa_=pa_: e.activation(out=gsil[:, tb, :], in_=pa_[:, 256:512], func=AF.Silu),
                  reads=[ra_], writes=[("gsil", tb)])
            sc.op("act", lambda e, tb=tb, pb_=pb_: e.activation(
                out=dgs[:, tb, :], in_=pb_[:, 256:384], func=AF.Silu), reads=[rb_], writes=[("dgs", tb)])
            sc.op("act", lambda e, tb=tb, pb_=pb_: e.activation(
                out=mgs[:, tb, :], in_=pb_[:, 384:512], func=AF.Silu), reads=[rb_], writes=[("mgs", tb)])
            sc.stage = 4.2
            sc.op("dve", lambda e, tb=tb, pa_=pa_: e.tensor_copy(out=gvb[:, tb, :], in_=pa_[:, 0:256]),
                  reads=[ra_], writes=[("gvb", tb)])
            sc.stage = 4.21
            sc.op("dve", lambda e, tb=tb, pb_=pb_: e.tensor_copy(out=gktok[:, tb, :], in_=pb_[:, 0:128]),
                  reads=[rb_], writes=[("gktok", tb)])
            sc.stage = 4.22
            sc.op("dve", lambda e, tb=tb, pb_=pb_, t=t: e.tensor_copy(out=V1[:, 4 * t + tb, 0:128], in_=pb_[:, 128:256]),
                  reads=[rb_], writes=[("V1", 4 * t + tb)])
            sc.stage = 4.3
            sc.op("pool", lambda e, tb=tb: e.tensor_tensor(out=gsil[:, tb, :], in0=gsil[:, tb, :], in1=glag[:], op=ALU.mult),
                  reads=[("gsil", tb), "glag"], writes=[("gsil", tb)])
            sc.op("pool", lambda e, tb=tb: e.tensor_tensor(out=dgs[:, tb, :], in0=dgs[:, tb, :], in1=dgg[:], op=ALU.mult),
                  reads=[("dgs", tb), "dgg"], writes=[("dgs", tb)])

        sc.stage = 5
        def fproj(name, ncols=128):
            p_, r_ = gbank()
            c0 = FB[name]
            def mm(e, p_=p_, c0=c0, ncols=ncols):
                ins = None
                for kc in range(NKC):
                    ins = e.matmul(p_[0:ncols, :], lhsT=WF[:, kc, c0:c0 + ncols], rhs=xb[:, kc, :], start=(kc == 0), stop=(kc == NKC - 1))
                return ins
            sc.op("pe", mm, reads=XB + WFres, writes=[r_])
            return p_, r_

        p_, r_ = fproj("lr", 16)
        sc.op("act", lambda e, p_=p_: e.activation(out=lrT[0:16, :], in_=p_[0:16, :], func=AF.Copy), reads=[r_], writes=["lrT"])
        pl, rl = gbank()
        def lg_mm(e, pl=pl):
            ins = None
            for c in range(4):
                ins = e.matmul(pl[:, c * 128:(c + 1) * 128], lhsT=lrT[0:33, c * 128:(c + 1) * 128], rhs=wup_s[0:33, :], start=True, stop=True)
            return ins
        sc.op("pe", lg_mm, reads=["lrT", "wup"], writes=[rl])
        sc.op("act", lambda e, pl=pl: e.activation(out=SP[:], in_=pl[:, :], func=AF.Exp, scale=-1.0), reads=[rl], writes=["SP"])
        sc.op("act", lambda e: e.activation(out=SP[:], in_=SP[:], func=AF.Ln, bias=col(CI_ONE)), reads=["SP", "cst"], writes=["SP"])

        for nm, sw, dst, dres in (("dq", "dqs", qT[:, :], "qT"), ("dk", "dks", KT[:, T0:T0 + 512], ("KT", t))):
            p_, r_ = fproj(nm)
            sc.op("dve", lambda e, p_=p_: e.tensor_tensor(out=T1[:], in0=p_[:, :], in1=cosT[:], op=ALU.mult),
                  reads=[r_, "cosT"], writes=["T1"])
            p2, r2 = fproj(sw)
            sc.op("dve", lambda e, p2=p2: e.tensor_tensor(out=T2[:], in0=p2[:, :], in1=sinS[:], op=ALU.mult),
                  reads=[r2, "sinS"], writes=["T2"])
            sc.op("pool", lambda e, dst=dst: e.tensor_tensor(out=dst, in0=T1[:], in1=T2[:], op=ALU.add),
                  reads=["T1", "T2"], writes=[dres])
        p_, r_ = fproj("mq")
        sc.op("act", lambda e, p_=p_: e.activation(out=mqT[:], in_=p_[:, :], func=AF.Copy), reads=[r_], writes=["mqT"])

        pbt, rbt = gbank()
        def bt_mm(e, pbt=pbt):
            ins = None
            for c in range(4):
                ins = e.matmul(pbt[:, c * 128:(c + 1) * 128], lhsT=SP[:, c * 128:(c + 1) * 128], rhs=tinc[:], start=True, stop=True)
            return ins
        sc.op("pe", bt_mm, reads=["SP", "tinc"], writes=[rbt])
        sc.op("act", lambda e, pbt=pbt: e.activation(out=EB[:], in_=pbt[:, :], func=AF.Exp), reads=[rbt], writes=["EB"])
        sc.op("act", lambda e, pbt=pbt: e.activation(out=ENB[:], in_=pbt[:, :], func=AF.Exp, scale=-1.0), reads=[rbt], writes=["ENB"])
        pd_, rd_ = gbank()
        def d_mm(e, pd_=pd_):
            ins = None
            for c in range(4):
                ins = e.matmul(pd_[:, c * 128:(c + 1) * 128], lhsT=texc[:], rhs=SP[:, c * 128:(c + 1) * 128], start=True, stop=True)
            return ins
        sc.op("pe", d_mm, reads=["SP", "texc"], writes=[rd_])
        sc.op("act", lambda e, pd_=pd_: e.activation(out=ED[:], in_=pd_[:, :], func=AF.Exp), reads=[rd_], writes=["ED"])
        sc.op("pool", lambda e: e.tensor_tensor(out=kdec[:].rearrange("p a b -> p (a b)"), in0=gktok[:].rearrange("p a b -> p (a b)"), in1=ED[:], op=ALU.mult),
              reads=[("gktok", i) for i in range(4)] + ["ED"], writes=["kdec"])

        p_, r_ = fproj("gq")
        sc.op("dve", lambda e, p_=p_: e.scalar_tensor_tensor(out=gqt[:], in0=p_[:, :], scalar=float(128 ** -0.5), in1=EB[:], op0=ALU.mult, op1=ALU.mult),
              reads=[r_, "EB"], writes=["gqt"])
        p_, r_ = fproj("gk")
        sc.op("dve", lambda e, p_=p_: e.tensor_tensor(out=gkt[:], in0=p_[:, :], in1=ENB[:], op=ALU.mult),
              reads=[r_, "ENB"], writes=["gkt"])

        sc.stage = 0
        if t + 1 < NT:
            issue_x(t + 1)
            issue_pos(t + 1)

        sc.stage = 6
        for c in range(4):
            cs = slice(c * 128, (c + 1) * 128)
            px, rx = gbank()
            py, ry = gbank()
            sc.op("pe", lambda e, px=px, cs=cs: e.matmul(px[:, 256:384], lhsT=gkt[:, cs], rhs=gqt[:, cs], start=True, stop=True),
                  reads=["gkt", "gqt"], writes=[rx])
            sc.op("dve", lambda e, px=px: e.tensor_tensor(out=sTm[:], in0=px[:, 256:384], in1=tri[:], op=ALU.mult),
                  reads=[rx, "tri"], writes=["sTm"])
            sc.op("pe", lambda e, py=py, c=c: e.matmul(py[:, 0:256], lhsT=kdec[:, c, :], rhs=gvb[:, c, :], start=True, stop=True),
                  reads=["kdec", ("gvb", c)], writes=[ry])
            sc.op("pe", lambda e, px=px, cs=cs: e.matmul(px[:, 0:256], lhsT=gqt[:, cs], rhs=Sbf[:], start=True, stop=False),
                  reads=["gqt", "Sbf"], writes=[rx])
            sc.op("pe", lambda e, px=px, c=c: e.matmul(px[:, 0:256], lhsT=sTm[:], rhs=gvb[:, c, :], start=False, stop=True),
                  reads=["sTm", ("gvb", c)], writes=[rx])
            sc.op("dve", lambda e, py=py, c=c: e.scalar_tensor_tensor(out=Sst[:], in0=Sst[:], scalar=EB[:, c * 128 + 127:c * 128 + 128], in1=py[:, 0:256], op0=ALU.mult, op1=ALU.add),
                  reads=["S", "EB", ry], writes=["S"])
            sc.op("pool", lambda e: e.tensor_copy(out=Sbf[:], in_=Sst[:]), reads=["S"], writes=["Sbf"])
            sc.op("act", lambda e, px=px: e.activation(out=junk[:], in_=px[:, 0:256], func=AF.Square, accum_out=sml[:, 0:1]),
                  reads=[rx], writes=["junk", "sml0"])
            sc.op("act", lambda e: e.activation(out=sml[:, 1:2], in_=sml[:, 0:1], func=AF.Ln, scale=float(1.0 / 256), bias=col(CI_EPSG)),
                  reads=["sml0", "cst"], writes=["sml1"])
            sc.op("act", lambda e: e.activation(out=sml[:, 2:3], in_=sml[:, 1:2], func=AF.Exp, scale=-0.5),
                  reads=["sml1"], writes=["sml2"])
            sc.op("dve", lambda e, px=px, c=c: e.scalar_tensor_tensor(out=mix[:, c, 0:256], in0=px[:, 0:256], scalar=sml[:, 2:3], in1=gsil[:, c, :], op0=ALU.mult, op1=ALU.mult),
                  reads=[rx, "sml2", ("gsil", c)], writes=[("mix", c, "g")])

        sc.stage = 7
        for mb in range(2):
            pq_, rq_ = gbank()
            sc.op("pe", lambda e, pq_=pq_, mb=mb: e.matmul(pq_[:, :], lhsT=mkT[:, mb * 128:(mb + 1) * 128], rhs=mqT[:], start=True, stop=True),
                  reads=["mkT", "mqT"], writes=[rq_])
            sc.op("act", lambda e, pq_=pq_, mb=mb: e.activation(out=pm[:, mb, :], in_=pq_[:, :], func=AF.Exp, scale=float(128 ** -0.5)),
                  reads=[rq_], writes=[("pm", mb)])
        for half in range(2):
            po_, ro_ = gbank()
            def mo_mm(e, po_=po_, half=half):
                ins = None
                for i in range(2):
                    tb = half * 2 + i
                    for mb in range(2):
                        ins = e.matmul(po_[:, i * 144:i * 144 + 129], lhsT=pm[:, mb, tb * 128:(tb + 1) * 128], rhs=mv1[:, mb, :], start=(mb == 0), stop=(mb == 1))
                return ins
            sc.op("pe", mo_mm, reads=[("pm", 0), ("pm", 1), "mv1", "mv1ones"], writes=[ro_])
            for i in range(2):
                tb = half * 2 + i
                sc.op("dve", lambda e, po_=po_, i=i: e.reciprocal(out=sml[:, 4:5], in_=po_[:, i * 144 + 128:i * 144 + 129]),
                      reads=[ro_], writes=["sml4"])
                sc.op("dve", lambda e, po_=po_, i=i, tb=tb: e.scalar_tensor_tensor(out=mix[:, tb, 384:512], in0=po_[:, i * 144:i * 144 + 128], scalar=sml[:, 4:5], in1=mgs[:, tb, :], op0=ALU.mult, op1=ALU.mult),
                      reads=[ro_, "sml4", ("mgs", tb)], writes=[("mix", tb, "m"), ro_])

        sc.stage = 8
        nkb = 4 * t + 4
        steps = [(kb, c) for kb in range(nkb) for c in range(2)]

        def issue_s(kb, c):
            qlo = max(0, kb - 4 * t) * 128
            ps = PS[psc[0] % 2]
            rps = ("PS", psc[0] % 2)
            psc[0] += 1
            pt = pT[ptc[0] % NPT]
            rpt = ("pT", ptc[0] % NPT)
            ptc[0] += 1
            sc.op("pe", lambda e, ps=ps, kb=kb, c=c, qlo=qlo: e.matmul(
                ps[:, qlo:512], lhsT=KT[c * 64:(c + 1) * 64, kb * 128:(kb + 1) * 128], rhs=qT[c * 64:(c + 1) * 64, qlo:512], start=True, stop=True),
                reads=[("KT", kb // 4), "qT"], writes=[rps])
            sc.op("act", lambda e, ps=ps, pt=pt, qlo=qlo: e.activation(out=pt[:, qlo:512], in_=ps[:, qlo:512], func=AF.Exp, scale=0.125),
                  reads=[rps], writes=[rpt])
            if kb >= 4 * t:
                sc.op("pool", lambda e, pt=pt, qlo=qlo: e.tensor_tensor(out=pt[:, qlo:qlo + 128], in0=pt[:, qlo:qlo + 128], in1=tri[:], op=ALU.mult),
                      reads=[rpt, "tri"], writes=[rpt])
            return pt, rpt, qlo

        def issue_pv(kb, c, pt, rpt, qlo):
            def pv(e, kb=kb, c=c, pt=pt, qlo=qlo):
                ins = None
                for qb in range(qlo // 128, 4):
                    reg, _ = pa_region(c, qb)
                    first = (kb == 0) and ((c * 4 + qb) % 3 == 0)
                    ins = e.matmul(reg, lhsT=pt[:, qb * 128:(qb + 1) * 128], rhs=V1[:, kb, :], start=first, stop=(kb == 4 * t + qb),
                                   skip_group_check=True)
                return ins
            sc.op("pe", pv, reads=[rpt, ("V1", kb), "V1ones"], writes=sorted(set(pa_region(c, qb)[1] for qb in range(qlo // 128, 4))))

        pend = issue_s(*steps[0])
        for si in range(len(steps)):
            nxt = issue_s(*steps[si + 1]) if si + 1 < len(steps) else None
            issue_pv(steps[si][0], steps[si][1], *pend)
            pend = nxt

        for qb in range(4):
            r1, rr1 = pa_region(0, qb)
            r2, rr2 = pa_region(1, qb)
            sc.op("dve", lambda e, r1=r1: e.reciprocal(out=sml[:, 6:7], in_=r1[:, 128:129]), reads=[rr1], writes=["sml6"])
            sc.op("dve", lambda e, r2=r2: e.reciprocal(out=sml[:, 7:8], in_=r2[:, 128:129]), reads=[rr2], writes=["sml7"])
            sc.op("dve", lambda e: e.tensor_tensor(out=sml[:, 8:9], in0=sml[:, 7:8], in1=lam[:], op=ALU.mult), reads=["sml7", "lam"], writes=["sml8"])
            sc.op("dve", lambda e, r2=r2: e.tensor_scalar(out=tmpd[:], in0=r2[:, 0:128], scalar1=sml[:, 8:9], scalar2=None, op0=ALU.mult),
                  reads=[rr2, "sml8"], writes=["tmpd"])
            sc.op("dve", lambda e, r1=r1: e.scalar_tensor_tensor(out=dif[:], in0=r1[:, 0:128], scalar=sml[:, 6:7], in1=tmpd[:], op0=ALU.mult, op1=ALU.subtract),
                  reads=[rr1, "sml6", "tmpd"], writes=["dif"])
            sc.op("act", lambda e: e.activation(out=junk[:, 0:128], in_=dif[:], func=AF.Square, accum_out=sml[:, 9:10]),
                  reads=["dif"], writes=["junk", "sml9"])
            sc.op("act", lambda e: e.activation(out=sml[:, 10:11], in_=sml[:, 9:10], func=AF.Ln, scale=float(1.0 / 128), bias=col(CI_EPSD)),
                  reads=["sml9", "cst"], writes=["sml10"])
            sc.op("act", lambda e: e.activation(out=sml[:, 11:12], in_=sml[:, 10:11], func=AF.Exp, scale=-0.5),
                  reads=["sml10"], writes=["sml11"])
            sc.op("dve", lambda e, qb=qb: e.scalar_tensor_tensor(out=mix[:, qb, 256:384], in0=dif[:], scalar=sml[:, 11:12], in1=dgs[:, qb, :], op0=ALU.mult, op1=ALU.mult),
                  reads=["dif", "sml11", ("dgs", qb)], writes=[("mix", qb, "d")])

        sc.stage = 9
        mixres = [("mix", tb, k) for tb in range(4) for k in ("g", "d", "m")]
        mix_stores.append(sc.dma("sp", lambda e, t=t: e.dma_start(out=mixv[:, 4 * t:4 * t + 4, :], in_=mix[:]), reads=mixres))

    sc.stage = 10
    cc = sc.collective(lambda e: e.collective_compute(
        "AllGather", mybir.AluOpType.bypass, replica_groups=[list(range(8))],
        ins=[mixbuf.ap().opt()], outs=[gath.ap().opt()]), reads=[], writes=["gath"])
    cc.waits.extend(o for o in mix_stores if o.token is not None)

    sc.stage = 11
    allkv = [("KT", i) for i in range(NT)] + [("V1", i) for i in range(S // 128)] + ["V1ones"]
    woutv = wout.rearrange("(k p) n -> p k n", p=128)
    for j in range(4):
        sc.dma("pool", lambda e, j=j: e.dma_start(out=WOUT[:, 4 * j:4 * j + 4, :], in_=woutv[:, 4 * j:4 * j + 4, :]),
               writes=[("WOUT", j)] + (allkv if j == 0 else []))
    WOres = [("WOUT", j) for j in range(4)]
    a2 = WTres + WFres
    sc.dma("sp", lambda e: e.dma_start(out=lng, in_=lng_d), writes=["lng"] + a2)
    sc.dma("sp", lambda e: e.dma_start(out=lnb, in_=lnb_d), writes=["lnb"])
    pidc = {}

    def pid_of(e, name):
        if name not in pidc:
            pidc[name] = e.partition_id()
        return pidc[name]

    gv = gath.ap()
    sel = nc.dram_tensor("sel", [4 * SQ, 512], BF16)
    selv = sel.ap().rearrange("(h s) c -> s h c", h=4)
    for h in range(4):
        def ldsel(e, h=h):
            pid = e.partition_id()
            row = ((pid // 4) * 4 + h) * S + (pid % 4) * SQ
            return e.dma_start(out=sel.ap()[h * SQ:(h + 1) * SQ, :], in_=gv[bass.ds(row, SQ), :])
        sc.dma("sp", ldsel, reads=["gath"], writes=[("sel", h)])
    alpha = float((2.0 * 1) ** 0.25)
    out_stores = []
    for tb in range(SQ // 128):
        bi = tb % 2
        sc.stage = 11.1
        sc.dma("pool", lambda e, tb=tb, bi=bi: e.dma_start(out=mrow[bi], in_=selv[tb * 128:(tb + 1) * 128, :, :]),
               reads=[("sel", h) for h in range(4)], writes=[("mrow", bi, h) for h in range(4)])
        sc.dma("sp", lambda e, tb=tb, bi=bi: e.dma_start(out=xrow[bi], in_=xq[tb * 128:(tb + 1) * 128, :]), writes=[("xrow", bi)] + [("yrow", bi, cb) for cb in range(4)])
        sc.stage = 11.2
        for h in range(4):
            ptb, rtb = gbank()
            def tr(e, h=h, bi=bi, ptb=ptb):
                ins = None
                for j in range(4):
                    ins = e.transpose(ptb[:, j * 128:(j + 1) * 128], mrow[bi][:, h, j * 128:(j + 1) * 128], ident[:])
                return ins
            sc.op("pe", tr, reads=[("mrow", bi, hh) for hh in range(4)] + ["ident"], writes=[rtb])
            sc.op("act", lambda e, h=h, bi=bi, ptb=ptb: e.activation(
                out=mixT[bi][:, h * 4:h * 4 + 4, :], in_=ptb[:, :].rearrange("p (k c) -> p k c", c=128), func=AF.Copy),
                reads=[rtb], writes=[("mixT", bi, h)])
        sc.stage = 11.3
        for cb in range(4):
            po_, ro_ = gbank()
            def omm(e, po_=po_, cb=cb, bi=bi):
                ins = None
                for fc in range(16):
                    ins = e.matmul(po_[:, :], lhsT=mixT[bi][:, fc, :], rhs=WOUT[:, fc, cb * 512:(cb + 1) * 512], start=(fc == 0), stop=(fc == 15))
                return ins
            sc.op("pe", omm, reads=[("mixT", bi, hh) for hh in range(4)] + WOres, writes=[ro_])
            sc.op("dve", lambda e, po_=po_, cb=cb, bi=bi: e.scalar_tensor_tensor(
                out=yrow[bi][:, cb * 512:(cb + 1) * 512], in0=xrow[bi][:, cb * 512:(cb + 1) * 512], scalar=alpha, in1=po_[:, :], op0=ALU.mult, op1=ALU.add),
                reads=[ro_, ("xrow", bi)], writes=[("yrow", bi, cb), ("xrow", bi)])
            sc.op("dve", lambda e, cb=cb, bi=bi: e.bn_stats(out=bnst[:, cb * 6:cb * 6 + 6], in_=yrow[bi][:, cb * 512:(cb + 1) * 512]),
                  reads=[("yrow", bi, cb)], writes=[("bnst", cb)])
        sc.stage = 11.4
        yres = [("yrow", bi, cb) for cb in range(4)]
        sc.op("dve", lambda e: e.bn_aggr(out=bnst[:, 24:26], in_=bnst[:, 0:24]), reads=[("bnst", cb) for cb in range(4)], writes=["mv"])
        sc.op("act", lambda e: e.activation(out=bnst[:, 26:27], in_=bnst[:, 25:26], func=AF.Ln, bias=col(CI_EPSL)), reads=["mv", "cst"], writes=["lnv"])
        sc.op("act", lambda e: e.activation(out=bnst[:, 27:28], in_=bnst[:, 26:27], func=AF.Exp, scale=-0.5), reads=["lnv"], writes=["rstd"])
        sc.op("dve", lambda e, bi=bi: e.tensor_scalar(out=yrow[bi], in0=yrow[bi], scalar1=bnst[:, 24:25], scalar2=bnst[:, 27:28], op0=ALU.subtract, op1=ALU.mult),
              reads=yres + ["mv", "rstd"], writes=yres)
        sc.op("pool", lambda e, bi=bi: e.tensor_tensor(out=yrow[bi], in0=yrow[bi], in1=lng, op=ALU.mult), reads=yres + ["lng"], writes=yres)
        sc.op("pool", lambda e, bi=bi: e.tensor_tensor(out=yrow[bi], in0=yrow[bi], in1=lnb, op=ALU.add), reads=yres + ["lnb"], writes=yres)
        out_stores.append(sc.dma("sp", lambda e, tb=tb, bi=bi: e.dma_start(out=y[tb * 128:(tb + 1) * 128, :], in_=yrow[bi]), reads=yres))

    sc.fence("sp", [o for o in out_stores if o.token is not None])
    with nc.allow_low_precision("bf16 matmul operands, fp32 accumulation"):
        sc.emit(nc, es)
    es.close()
    return nc


def _prep_inputs(x, mem, positions, w_in, w_gk_up, b_gk_up, gla_norm_g, lambda_q1, lambda_k1,
                 lambda_q2, lambda_k2, diff_norm_g, w_mem_kv, w_out, ln_g, ln_b):
    f32 = np.float32
    x = np.asarray(x, f32)
    mem = np.asarray(mem, f32)
    positions = np.asarray(positions, np.int32)
    w_in = np.asarray(w_in, f32)[0]
    w_up = np.asarray(w_gk_up, f32)[0]
    b_up = np.asarray(b_gk_up, f32)[0]
    w_mem = np.asarray(w_mem_kv, f32)[0]
    w_o = np.asarray(w_out, f32)[0]

    def rep(v, n=128):
        return np.ascontiguousarray(np.broadcast_to(np.asarray(v, f32).reshape(1, -1), (n, np.asarray(v).size)))

    half = (np.arange(0, 16, 2, dtype=f32) / f32(16)).astype(f32)
    inv_freq = np.power(f32(500000.0), -half).astype(f32)
    cst = np.zeros((128, 8), f32)
    for p in range(128):
        d = p % 64
        if d < 16:
            cst[p, CI_INVF] = inv_freq[d % 8]
            cst[p, CI_SGN] = -1.0 if d < 8 else 1.0
            cst[p, CI_NOTROT] = -2.0
        else:
            cst[p, CI_NOTROT] = 1.0
    cst[:, CI_HALFPI] = math.pi / 2
    cst[:, CI_ONE] = 1.0
    cst[:, CI_EPSG] = 1e-6
    cst[:, CI_EPSD] = 1e-5
    cst[:, CI_EPSL] = 1e-5
    jj, ii = np.meshgrid(np.arange(128), np.arange(128), indexing="ij")
    tri = (jj <= ii).astype(f32)
    tinc = tri * f32(-1.0 / 16)
    texc = (jj > ii).astype(f32) * f32(-1.0 / 16)
    ident = np.eye(128, dtype=f32)
    swap = np.arange(128)
    for p in range(128):
        d = p % 64
        if d < 8:
            swap[p] = p + 8
        elif d < 16:
            swap[p] = p - 8

    rows = []
    for h in range(4):
        rows.append(np.arange(h * 256, h * 256 + 256))
        rows.append(np.arange(1024 + h * 128, 1024 + h * 128 + 128))
        rows.append(np.arange(1536 + h * 128, 1536 + h * 128 + 128))
    woutP = np.ascontiguousarray(w_o[np.concatenate(rows)])
    lamp = rep(np.concatenate([np.asarray(a, f32).reshape(-1) for a in (lambda_q1, lambda_k1, lambda_q2, lambda_k2)]))
    glag = rep(np.asarray(gla_norm_g, f32).reshape(-1))
    dgg = rep(np.asarray(diff_norm_g, f32).reshape(-1))
    lng = rep(np.asarray(ln_g, f32).reshape(-1))
    lnb = rep(np.asarray(ln_b, f32).reshape(-1))

    xTs = [np.ascontiguousarray(x[b].T) for b in range(2)]
    memTs = [np.ascontiguousarray(mem[b].T) for b in range(2)]
    posrs = [np.ascontiguousarray(np.broadcast_to(positions[b][None, :], (128, S))) for b in range(2)]
    in_maps = []
    for c in range(8):
        b, h = c // 4, c % 4
        gq = w_in[:, 0 + h * 128:0 + h * 128 + 128]
        gk = w_in[:, 512 + h * 128:512 + h * 128 + 128]
        gv = w_in[:, 1024 + h * 256:1024 + h * 256 + 256]
        gg = w_in[:, 2048 + h * 256:2048 + h * 256 + 256]
        lr = w_in[:, 3072:3088]
        dq = w_in[:, 3088 + h * 128:3088 + h * 128 + 128]
        dk = w_in[:, 3600 + h * 128:3600 + h * 128 + 128]
        dv = w_in[:, 4112 + h * 128:4112 + h * 128 + 128]
        dg = w_in[:, 4624 + h * 128:4624 + h * 128 + 128]
        mq = w_in[:, 5136 + h * 128:5136 + h * 128 + 128]
        mg = w_in[:, 5648 + h * 128:5648 + h * 128 + 128]
        wF = np.ascontiguousarray(np.concatenate([gq, gk, dq, dk, dq[:, swap], dk[:, swap], mq, lr], axis=1))
        wT = np.ascontiguousarray(np.concatenate([gv, gg, gk, dv, dg, mg], axis=1))
        wm = np.ascontiguousarray(np.concatenate([w_mem[:, h * 128:h * 128 + 128], w_mem[:, 512 + h * 128:512 + h * 128 + 128]], axis=1))
        wup = np.zeros((33, 128), f32)
        wup[0:16] = w_up[:, h * 128:h * 128 + 128]
        wup[32] = b_up[h * 128:h * 128 + 128]
        in_maps.append({
            "xT": xTs[b], "xq": np.ascontiguousarray(x[b, h * SQ:(h + 1) * SQ, :]),
            "wF": wF, "wT": wT, "memT": memTs[b], "wm": wm, "wup": wup, "wout": woutP,
            "posr": posrs[b], "glag": glag, "dgg": dgg, "lamp": lamp, "lng": lng, "lnb": lnb,
            "cst": cst, "tri": tri, "tinc": tinc, "texc": texc, "ident": ident,
        })
    return in_maps


_NC_CACHE = {}


def _set_S(s_):
    global S, NT, SQ
    S = s_
    NT = S // 512
    SQ = S // 4


def kernel(**inputs):
    in_maps = _prep_inputs(**inputs)
    if "nc" not in _NC_CACHE:
        _NC_CACHE["nc"] = build_nc()
    nc = _NC_CACHE["nc"]
    res = run_bass_kernel_spmd(nc, in_maps, core_ids=list(range(8)))
    out = np.empty((2, S, D), np.float32)
    for c in range(8):
        b, h = c // 4, c % 4
        out[b, h * SQ:(h + 1) * SQ, :] = res.results[c]["y"]
    return out
```

```python
import math
from contextlib import ExitStack

import numpy as np
import concourse.bass as bass
import concourse.mybir as mybir
from concourse.bass_utils import run_bass_kernel_spmd

F32 = mybir.dt.float32
BF16 = mybir.dt.bfloat16
I32 = mybir.dt.int32
AF = mybir.ActivationFunctionType
ALU = mybir.AluOpType

D = 2048
S = 16384
NT = S // 512
NKC = D // 128
SQ = S // 4
WF_N = 7 * 128 + 16
WT_N = 1024
LAM_INIT = 0.8 - 0.6 * math.exp(-0.3 * 0)
TWO_PI = 2.0 * math.pi
C1 = 6.28125
C2 = TWO_PI - C1

FB = {"gq": 0, "gk": 128, "dq": 256, "dk": 384, "dqs": 512, "dks": 640, "mq": 768, "lr": 896}

CI_INVF, CI_SGN, CI_NOTROT, CI_HALFPI, CI_ONE, CI_EPSG, CI_EPSD, CI_EPSL = range(8)


class Op:
    __slots__ = ("eng", "fn", "waits", "signal", "token", "inc", "sigidx")

    def __init__(self, eng, fn):
        self.eng = eng
        self.fn = fn
        self.waits = []
        self.signal = False
        self.token = None
        self.inc = 1
        self.sigidx = None


class Sched:
    ENGS = ["pe", "act", "dve", "pool", "sp"]
    NSLOT = 8

    def __init__(self):
        self.ops = {e: [] for e in self.ENGS}
        self.last_writer = {}
        self.readers = {}
        self.slot_prev = {}
        self.slot_uses = {}
        self.next_slot = {"sp": 0, "pool": 0}
        self.dma_sems = {}
        self.eng_sems = {}
        self.cc_sem = None

    @staticmethod
    def _is_psum(r):
        return isinstance(r, tuple) and r[0] in ("PG", "PS", "PAB")

    def _deps(self, op, reads, writes):
        writes = list(writes) + [r for r in reads if self._is_psum(r)]
        reads = [r for r in reads if not self._is_psum(r)]
        deps = []
        for r in reads:
            w = self.last_writer.get(r)
            if w is not None:
                deps.append(w)
        for w_ in writes:
            w = self.last_writer.get(w_)
            if w is not None:
                deps.append(w)
            deps.extend(self.readers.get(w_, ()))
        for w_ in writes:
            self.last_writer[w_] = op
            self.readers[w_] = []
        for r in reads:
            lst = self.readers.setdefault(r, [])
            if op.token is None:
                lst[:] = [o for o in lst if not (o.eng == op.eng and o.token is None)]
            lst.append(op)
        seen = set()
        for d in deps:
            if d is op or id(d) in seen:
                continue
            seen.add(id(d))
            if d.eng == "pe" and op.eng == "pe" and d.token is None and op.token is None:
                continue
            if d.token is None:
                d.signal = True
            op.waits.append(d)

    stage = 0
    level = 99

    def op(self, eng, fn, reads=(), writes=()):
        o = Op(eng, fn)
        if self.stage > self.level:
            return o
        self._deps(o, reads, writes)
        self.ops[eng].append(o)
        return o

    def dma(self, q, fn, reads=(), writes=()):
        o = Op(q, fn)
        if self.stage > self.level:
            return o
        slot = self.next_slot[q] % self.NSLOT
        self.next_slot[q] += 1
        key = (q, slot)
        uses = self.slot_uses.get(key, 0) + 1
        self.slot_uses[key] = uses
        o.token = (("dma", q, slot), 16 * uses)
        o.inc = 16
        prev = self.slot_prev.get(key)
        if prev is not None:
            o.waits.append(prev)
        self.slot_prev[key] = o
        self._deps(o, reads, writes)
        self.ops[q].append(o)
        return o

    def collective(self, fn, reads=(), writes=()):
        o = Op("pool", fn)
        if self.stage > self.level:
            return o
        o.token = (("cc",), 1)
        o.inc = 1
        self._deps(o, reads, writes)
        self.ops["pool"].append(o)
        return o

    def fence(self, eng, deps):
        o = Op(eng, None)
        for d in deps:
            if d.token is None and d.sigidx is None and not any(d is x for x in self.ops[d.eng]):
                continue
            if d.token is None:
                d.signal = True
            o.waits.append(d)
        self.ops[eng].append(o)
        return o

    def emit(self, nc, es):
        for e in self.ENGS:
            self.eng_sems[e] = es.enter_context(nc.semaphore("sem_" + e))
        for q in ("sp", "pool"):
            for s in range(self.NSLOT):
                self.dma_sems[("dma", q, s)] = es.enter_context(nc.semaphore("dsem_%s_%d" % (q, s)))
        self.cc_sem = es.enter_context(nc.semaphore("ccsem"))
        for e in self.ENGS:
            n = 0
            for o in self.ops[e]:
                if o.token is None and o.signal:
                    n += 1
                    o.sigidx = n
        block = es.enter_context(nc.Block())

        def tok(o):
            if o.token is None:
                return ("eng", o.eng), o.sigidx
            return o.token

        def semof(k):
            if k[0] == "eng":
                return self.eng_sems[k[1]]
            if k[0] == "cc":
                return self.cc_sem
            return self.dma_sems[k]

        def run(e):
            def body(eng):
                known = {}
                for o in self.ops[e]:
                    for d in o.waits:
                        k, v = tok(d)
                        if known.get(k, 0) < v:
                            eng.wait_ge(semof(k), v)
                            known[k] = v
                    if o.fn is None:
                        continue
                    ins = o.fn(eng)
                    if o.token is not None:
                        if o.token[0][0] == "cc":
                            ins.then_inc(self.cc_sem)
                        else:
                            ins.then_inc(self.dma_sems[o.token[0]], 16)
                    elif o.signal:
                        ins.then_inc(self.eng_sems[e], 1)
            return body

        block.tensor(run("pe"))
        block.scalar(run("act"))
        block.vector(run("dve"))
        block.gpsimd(run("pool"))
        block.sync(run("sp"))


def build_nc(level=99):
    nc = bass.Bass("TRN2", target_bir_lowering=False)
    es = ExitStack()
    sc = Sched()
    sc.level = level

    def dram_in(name, shape, dt=F32):
        return nc.dram_tensor(name, list(shape), dt, kind="ExternalInput").ap()

    xT = dram_in("xT", [D, S])
    xq = dram_in("xq", [SQ, D])
    wF = dram_in("wF", [D, WF_N])
    wT = dram_in("wT", [D, WT_N])
    memT = dram_in("memT", [D, 256])
    wm = dram_in("wm", [D, 256])
    wup = dram_in("wup", [33, 128])
    wout = dram_in("wout", [D, D])
    posr = dram_in("posr", [128, S], I32)
    glag_d = dram_in("glag", [128, 256])
    dgg_d = dram_in("dgg", [128, 128])
    lamp_d = dram_in("lamp", [128, 256])
    lng_d = dram_in("lng", [128, D])
    lnb_d = dram_in("lnb", [128, D])
    cst_d = dram_in("cst", [128, 8])
    tri_d = dram_in("tri", [128, 128])
    tinc_d = dram_in("tinc", [128, 128])
    texc_d = dram_in("texc", [128, 128])
    ident_d = dram_in("ident", [128, 128])
    y = nc.dram_tensor("y", [SQ, D], F32, kind="ExternalOutput").ap()
    mixbuf = nc.dram_tensor("mixbuf", [S, 512], BF16)
    gath = nc.dram_tensor("gath", [8 * S, 512], BF16)

    def sb(name, shape, dt):
        return es.enter_context(nc.sbuf_tensor("s_" + name, list(shape), dt))

    arena = sb("arena", [128, 16384 + 128 * 129], BF16)
    KT = arena[:, 0:S]
    V1 = arena[:, 16384:16384 + (S // 128) * 129].rearrange("p (k c) -> p k c", c=129)
    WOUT = arena[:, 0:32768].rearrange("p (k n) -> p k n", n=2048)
    arena2 = sb("arena2", [128, NKC * (WF_N + WT_N)], BF16)
    WF = arena2[:, 0:NKC * WF_N].rearrange("p (k n) -> p k n", n=WF_N)
    WT = arena2[:, NKC * WF_N:NKC * (WF_N + WT_N)].rearrange("p (k n) -> p k n", n=WT_N)
    xb = sb("xb", [128, NKC, 512], BF16)
    memTb = xb[:, :, 0:256]
    wmb = xb[:, :, 256:512]
    mkT = sb("mkT", [128, 256], BF16)
    mv1 = sb("mv1", [128, 2, 129], BF16)
    wup_s = sb("wup_s", [33, 128], F32)
    cst = sb("cst", [128, 8], F32)
    tri = sb("tri", [128, 128], BF16)
    tinc = sb("tinc", [128, 128], F32)
    texc = sb("texc", [128, 128], F32)
    ident = sb("ident", [128, 128], F32)
    glag = sb("glag_s", [128, 256], F32)
    dgg = sb("dgg_s", [128, 128], F32)
    lamp = sb("lamp_s", [128, 256], F32)
    lam = sb("lam", [128, 1], F32)
    lamt = sb("lamt", [128, 4], F32)
    Sst = sb("Sst", [128, 256], F32)
    Sbf = sb("Sbf", [128, 256], BF16)
    bufK = sb("bufK", [128, 512], I32)
    posi = bufK
    bufR = sb("bufR", [128, 512], F32)
    sinS = sb("sinS", [128, 512], F32)
    cosT = sb("cosT", [128, 512], F32)
    T1 = sb("T1", [128, 512], F32)
    T2 = sb("T2", [128, 512], F32)
    qT2s = [sb("qT2_%d" % i, [128, 2, 512], BF16) for i in range(2)]
    gqt = sb("gqt", [128, 512], BF16)
    gkt = sb("gkt", [128, 512], BF16)
    mqT = sb("mqT", [128, 512], BF16)
    lrT = sb("lrT", [33, 512], F32)
    SP = sb("SP", [128, 512], F32)
    EB = sb("EB", [128, 512], F32)
    ENB = sb("ENB", [128, 512], F32)
    gktok = sb("gktok", [128, 4, 128], F32)
    kdec = sb("kdec", [128, 4, 128], BF16)
    gvb = sb("gvb", [128, 4, 256], BF16)
    gsil = sb("gsil", [128, 4, 256], F32)
    dgss = [sb("dgs%d" % i, [128, 4, 128], F32) for i in range(2)]
    mgs = sb("mgs", [128, 4, 128], F32)
    sTm = sb("sTm", [128, 128], BF16)
    junk = sb("junk", [128, 256], F32)
    junk2 = sb("junk2", [128, 128], F32)
    sml = sb("sml", [128, 16], F32)
    pm = sb("pm", [128, 2, 512], BF16)
    NPT = 4
    pT = [sb("pT%d" % i, [128, 512], BF16) for i in range(NPT)]
    tmpd = sb("tmpd", [128, 128], F32)
    dif = sb("dif", [128, 128], F32)
    mixs = [sb("mix%d" % i, [128, 4, 512], BF16) for i in range(2)]

    off = [0]

    def carve(nelem_bf16):
        a = arena2[:, off[0]:off[0] + nelem_bf16]
        off[0] += nelem_bf16
        return a

    mrow = [carve(2 * 4 * 512).bitcast(F32).rearrange("p (h c) -> p h c", c=512) for _ in range(2)]
    mixT = [carve(16 * 128).rearrange("p (k c) -> p k c", c=128) for _ in range(2)]
    xrow = [carve(2 * D).bitcast(F32) for _ in range(2)]
    yrow = xrow
    lng = carve(2 * D).bitcast(F32)
    lnb = carve(2 * D).bitcast(F32)
    bnst = carve(2 * 32).bitcast(F32)
    assert off[0] <= NKC * (WF_N + WT_N)

    PB = [es.enter_context(nc.psum_tensor("pb%d" % i, [128, 512], F32)) for i in range(8)]
    PG = PB[0:3]
    PS = PB[3:5]
    PA = PB[5:8]
    gctr = [0]

    def gbank():
        i = gctr[0] % 3
        gctr[0] += 1
        return PG[i], ("PG", i)

    def pa_region(c, qb):
        idx = c * 4 + qb
        return PA[idx // 3][:, (idx % 3) * 144:(idx % 3) * 144 + 129], ("PAB", idx // 3)

    def col(i):
        return cst[:, i:i + 1]

    def ld_sp(dst, src, res):
        return sc.dma("sp", lambda e, d=dst, s=src: e.dma_start(out=d, in_=s), writes=[res])

    def ld_cast(dst, src, res):
        return sc.dma("pool", lambda e, d=dst, s=src: e.dma_start(out=d, in_=s), writes=[res])

    ld_sp(cst[:], cst_d, "cst")
    ld_sp(tinc[:], tinc_d, "tinc")
    ld_sp(texc[:], texc_d, "texc")
    ld_sp(wup_s[:], wup, "wup")
    ld_sp(glag[:], glag_d, "glag")
    ld_sp(dgg[:], dgg_d, "dgg")
    ld_sp(lamp[:], lamp_d, "lamp")
    ld_cast(tri[:], tri_d, "tri")
    ld_sp(ident[:], ident_d, "ident")
    XB0 = [("xb", j) for j in range(4)]
    sc.dma("pool", lambda e: e.dma_start(out=memTb, in_=memT.rearrange("(k p) m -> p k m", p=128)), writes=XB0)
    sc.dma("pool", lambda e: e.dma_start(out=wmb, in_=wm.rearrange("(k p) m -> p k m", p=128)), writes=XB0)
    wTv = wT.rearrange("(k p) n -> p k n", p=128)
    wFv = wF.rearrange("(k p) n -> p k n", p=128)
    for j in range(4):
        ld_cast(WT[:, 4 * j:4 * j + 4, :], wTv[:, 4 * j:4 * j + 4, :], ("WT", j))
    for j in range(4):
        ld_cast(WF[:, 4 * j:4 * j + 4, :], wFv[:, 4 * j:4 * j + 4, :], ("WF", j))
    WTres = [("WT", j) for j in range(4)]
    WFres = [("WF", j) for j in range(4)]

    sc.op("pool", lambda e: e.tensor_scalar(out=dgg[:], in0=dgg[:], scalar1=float(1.0 - LAM_INIT), scalar2=None, op0=ALU.mult),
          reads=["dgg"], writes=["dgg"])
    for i in range(2):
        sc.op("pool", lambda e, i=i: e.memset(qT2s[i][:], 0.0), writes=[("qT2", i)])
    sc.op("pool", lambda e: e.memset(Sst[:], 0.0), writes=["S"])
    sc.op("pool", lambda e: e.memset(Sbf[:], 0.0), writes=["Sbf"])
    sc.op("pool", lambda e: e.memset(lrT[:], 0.0), writes=["lrT"])
    sc.op("pool", lambda e: e.memset(lrT[32:33, :], 1.0), writes=["lrT"])
    sc.op("pool", lambda e: e.memset(V1[:, :, 128:129], 1.0), writes=["V1ones"])
    sc.op("pool", lambda e: e.memset(mv1[:, :, 128:129], 1.0), writes=["mv1ones"])

    sc.stage = 1
    sc.op("dve", lambda e: e.tensor_tensor(out=junk[:, 0:64], in0=lamp[:, 0:64], in1=lamp[:, 64:128], op=ALU.mult),
          reads=["lamp"], writes=["junk"])
    sc.op("dve", lambda e: e.tensor_reduce(out=lamt[:, 0:1], in_=junk[:, 0:64], axis=mybir.AxisListType.X, op=ALU.add),
          reads=["junk"], writes=["lamt0"])
    sc.op("dve", lambda e: e.tensor_tensor(out=junk[:, 64:128], in0=lamp[:, 128:192], in1=lamp[:, 192:256], op=ALU.mult),
          reads=["lamp"], writes=["junk2"])
    sc.op("dve", lambda e: e.tensor_reduce(out=lamt[:, 1:2], in_=junk[:, 64:128], axis=mybir.AxisListType.X, op=ALU.add),
          reads=["junk2"], writes=["lamt1"])
    sc.op("act", lambda e: e.activation(out=lamt[:, 2:4], in_=lamt[:, 0:2], func=AF.Exp),
          reads=["lamt0", "lamt1"], writes=["lamt2"])
    sc.op("dve", lambda e: e.tensor_tensor(out=lam[:], in0=lamt[:, 2:3], in1=lamt[:, 3:4], op=ALU.subtract),
          reads=["lamt2"], writes=["lam"])
    sc.op("dve", lambda e: e.tensor_scalar(out=lam[:], in0=lam[:], scalar1=float(LAM_INIT), scalar2=None, op0=ALU.add),
          reads=["lam"], writes=["lam"])

    sc.stage = 2
    pk, rk = gbank()
    def mk_mm(e):
        ins = None
        for kc in range(NKC):
            ins = e.matmul(pk[:, 0:256], lhsT=wmb[:, kc, 0:128], rhs=memTb[:, kc, :], start=(kc == 0), stop=(kc == NKC - 1))
        return ins
    sc.op("pe", mk_mm, reads=XB0, writes=[rk])
    sc.op("act", lambda e: e.activation(out=mkT[:], in_=pk[:, 0:256], func=AF.Copy), reads=[rk], writes=["mkT"])
    pv_, rv_ = gbank()
    def mv_mm(e):
        ins = None
        for mb in range(2):
            for kc in range(NKC):
                ins = e.matmul(pv_[:, mb * 128:(mb + 1) * 128], lhsT=memTb[:, kc, mb * 128:(mb + 1) * 128],
                               rhs=wmb[:, kc, 128:256], start=(kc == 0), stop=(kc == NKC - 1))
        return ins
    sc.op("pe", mv_mm, reads=XB0, writes=[rv_])
    sc.op("act", lambda e: e.activation(out=mv1[:, :, 0:128], in_=pv_[:, 0:256].rearrange("p (m d) -> p m d", d=128), func=AF.Copy),
          reads=[rv_], writes=["mv1"])

    xTv = xT.rearrange("(k p) s -> p k s", p=128)
    mixv = mixbuf.ap().rearrange("(n p) c -> p n c", p=128)
    mix_stores = []
    ptc = [0]
    psc = [0]

    def issue_x(t):
        T0 = t * 512
        for j in range(4):
            sc.dma("pool", lambda e, j=j, T0=T0: e.dma_start(out=xb[:, 4 * j:4 * j + 4, :], in_=xTv[:, 4 * j:4 * j + 4, T0:T0 + 512]),
                   writes=[("xb", j)])

    def issue_pos(t):
        T0 = t * 512
        sc.dma("sp", lambda e, T0=T0: e.dma_start(out=posi[:], in_=posr[:, T0:T0 + 512]), writes=["posK"])

    XB = [("xb", j) for j in range(4)]

    def front(t):
        T0 = t * 512
        pb = t % 2
        qT2 = qT2s[pb]
        mix = mixs[pb]
        dgs = dgss[pb]
        sc.stage = 3
        A, K, R, M = T1, bufK, bufR, T2
        sc.op("dve", lambda e: e.tensor_copy(out=A[:], in_=bufK[:]), reads=["posK"], writes=["T1"])
        sc.op("dve", lambda e: e.tensor_scalar(out=A[:], in0=A[:], scalar1=col(CI_INVF), scalar2=None, op0=ALU.mult),
              reads=["T1", "cst"], writes=["T1"])
        sc.op("dve", lambda e: e.tensor_scalar(out=K[:], in0=A[:], scalar1=float(1.0 / TWO_PI), scalar2=None, op0=ALU.mult),
              reads=["T1"], writes=["posK"])
        sc.op("dve", lambda e: e.scalar_tensor_tensor(out=R[:], in0=K[:], scalar=float(-C1), in1=A[:], op0=ALU.mult, op1=ALU.add),
              reads=["posK", "T1"], writes=["bufR"])
        sc.op("dve", lambda e: e.scalar_tensor_tensor(out=R[:], in0=K[:], scalar=float(-C2), in1=R[:], op0=ALU.mult, op1=ALU.add),
              reads=["posK", "bufR"], writes=["bufR"])
        if t + 1 < NT:
            sc.stage = 0
            issue_pos(t + 1)
            sc.stage = 3
        sc.op("dve", lambda e: e.tensor_scalar(out=M[:], in0=R[:], scalar1=float(math.pi), scalar2=float(-TWO_PI), op0=ALU.is_gt, op1=ALU.mult),
              reads=["bufR"], writes=["T2"])
        sc.op("dve", lambda e: e.tensor_tensor(out=sinS[:], in0=M[:], in1=R[:], op=ALU.add),
              reads=["T2", "bufR"], writes=["sinS"])
        sc.op("dve", lambda e: e.tensor_scalar(out=M[:], in0=R[:], scalar1=float(math.pi / 2), scalar2=float(-TWO_PI), op0=ALU.is_gt, op1=ALU.mult),
              reads=["bufR"], writes=["T2"])
        sc.op("dve", lambda e: e.tensor_tensor(out=cosT[:], in0=M[:], in1=R[:], op=ALU.add),
              reads=["T2", "bufR"], writes=["cosT"])
        sc.op("act", lambda e: e.activation(out=sinS[:], in_=sinS[:], func=AF.Sin), reads=["sinS"], writes=["sinS"])
        sc.op("act", lambda e: e.activation(out=cosT[:], in_=cosT[:], func=AF.Sin, bias=col(CI_HALFPI)), reads=["cosT", "cst"], writes=["cosT"])
        sc.op("dve", lambda e: e.tensor_scalar(out=sinS[:], in0=sinS[:], scalar1=col(CI_SGN), scalar2=None, op0=ALU.mult),
              reads=["sinS", "cst"], writes=["sinS"])
        sc.op("dve", lambda e: e.tensor_scalar(out=cosT[:], in0=cosT[:], scalar1=col(CI_NOTROT), scalar2=None, op0=ALU.max),
              reads=["cosT", "cst"], writes=["cosT"])

        sc.stage = 4
        for tb in range(4):
            pa_, ra_ = gbank()
            pb_, rb_ = gbank()
            def tproj(e, tb=tb, pa_=pa_, pb_=pb_):
                ins = None
                for kc in range(NKC):
                    lt = xb[:, kc, tb * 128:(tb + 1) * 128]
                    e.matmul(pa_[:, :], lhsT=lt, rhs=WT[:, kc, 0:512], start=(kc == 0), stop=(kc == NKC - 1))
                    ins = e.matmul(pb_[:, :], lhsT=lt, rhs=WT[:, kc, 512:1024], start=(kc == 0), stop=(kc == NKC - 1))
                return ins
            sc.op("pe", tproj, reads=XB + WTres, writes=[ra_, rb_])
            sc.op("act", lambda e, tb=tb, pa_=pa_: e.activation(out=gsil[:, tb, :], in_=pa_[:, 256:512], func=AF.Silu),
                  reads=[ra_], writes=[("gsil", tb)])
            sc.op("act", lambda e, tb=tb, pb_=pb_: e.activation(
                out=dgs[:, tb, :], in_=pb_[:, 256:384], func=AF.Silu), reads=[rb_], writes=[("dgs", pb, tb)])
            sc.op("act", lambda e, tb=tb, pb_=pb_: e.activation(
                out=mgs[:, tb, :], in_=pb_[:, 384:512], func=AF.Silu), reads=[rb_], writes=[("mgs", tb)])
            sc.op("dve", lambda e, tb=tb, pa_=pa_: e.tensor_copy(out=gvb[:, tb, :], in_=pa_[:, 0:256]),
                  reads=[ra_], writes=[("gvb", tb)])
            sc.op("dve", lambda e, tb=tb, pb_=pb_: e.tensor_copy(out=gktok[:, tb, :], in_=pb_[:, 0:128]),
                  reads=[rb_], writes=[("gktok", tb)])
            sc.op("dve", lambda e, tb=tb, pb_=pb_, t=t: e.tensor_copy(out=V1[:, 4 * t + tb, 0:128], in_=pb_[:, 128:256]),
                  reads=[rb_], writes=[("V1", 4 * t + tb)])
            sc.op("pool", lambda e, tb=tb: e.tensor_tensor(out=gsil[:, tb, :], in0=gsil[:, tb, :], in1=glag[:], op=ALU.mult),
                  reads=[("gsil", tb), "glag"], writes=[("gsil", tb)])
            sc.op("pool", lambda e, tb=tb: e.tensor_tensor(out=dgs[:, tb, :], in0=dgs[:, tb, :], in1=dgg[:], op=ALU.mult),
                  reads=[("dgs", pb, tb), "dgg"], writes=[("dgs", pb, tb)])
            yield

        sc.stage = 5
        def fproj(name, ncols=128):
            p_, r_ = gbank()
            c0 = FB[name]
            def mm(e, p_=p_, c0=c0, ncols=ncols):
                ins = None
                for kc in range(NKC):
                    ins = e.matmul(p_[0:ncols, :], lhsT=WF[:, kc, c0:c0 + ncols], rhs=xb[:, kc, :], start=(kc == 0), stop=(kc == NKC - 1))
                return ins
            sc.op("pe", mm, reads=XB + WFres, writes=[r_])
            return p_, r_

        p_, r_ = fproj("lr", 16)
        sc.op("act", lambda e, p_=p_: e.activation(out=lrT[0:16, :], in_=p_[0:16, :], func=AF.Copy), reads=[r_], writes=["lrT"])
        yield
        for nm, sw in (("dq", "dqs"), ("dk", "dks")):
            p_, r_ = fproj(nm)
            sc.op("dve", lambda e, p_=p_: e.tensor_tensor(out=T1[:], in0=p_[:, :], in1=cosT[:], op=ALU.mult),
                  reads=[r_, "cosT"], writes=["T1"])
            yield
            p2, r2 = fproj(sw)
            sc.op("dve", lambda e, p2=p2: e.tensor_tensor(out=T2[:], in0=p2[:, :], in1=sinS[:], op=ALU.mult),
                  reads=[r2, "sinS"], writes=["T2"])
            if nm == "dq":
                for c in range(2):
                    sc.op("pool", lambda e, c=c: e.tensor_tensor(out=qT2[c * 64:(c + 1) * 64, c, :], in0=T1[c * 64:(c + 1) * 64, :], in1=T2[c * 64:(c + 1) * 64, :], op=ALU.add),
                          reads=["T1", "T2"], writes=[("qT2", pb)])
            else:
                sc.op("pool", lambda e, T0=T0: e.tensor_tensor(out=KT[:, T0:T0 + 512], in0=T1[:], in1=T2[:], op=ALU.add),
                      reads=["T1", "T2"], writes=[("KT", t)])
            yield
        pl, rl = gbank()
        def lg_mm(e, pl=pl):
            ins = None
            for c in range(4):
                ins = e.matmul(pl[:, c * 128:(c + 1) * 128], lhsT=lrT[0:33, c * 128:(c + 1) * 128], rhs=wup_s[0:33, :], start=True, stop=True)
            return ins
        sc.op("pe", lg_mm, reads=["lrT", "wup"], writes=[rl])
        sc.op("act", lambda e, pl=pl: e.activation(out=SP[:], in_=pl[:, :], func=AF.Exp, scale=-1.0), reads=[rl], writes=["SP"])
        sc.op("act", lambda e: e.activation(out=SP[:], in_=SP[:], func=AF.Ln, bias=col(CI_ONE)), reads=["SP", "cst"], writes=["SP"])
        p_, r_ = fproj("mq")
        sc.op("act", lambda e, p_=p_: e.activation(out=mqT[:], in_=p_[:, :], func=AF.Copy), reads=[r_], writes=["mqT"])
        yield
        pbt, rbt = gbank()
        def bt_mm(e, pbt=pbt):
            ins = None
            for c in range(4):
                ins = e.matmul(pbt[:, c * 128:(c + 1) * 128], lhsT=SP[:, c * 128:(c + 1) * 128], rhs=tinc[:], start=True, stop=True)
            return ins
        sc.op("pe", bt_mm, reads=["SP", "tinc"], writes=[rbt])
        sc.op("act", lambda e, pbt=pbt: e.activation(out=EB[:], in_=pbt[:, :], func=AF.Exp), reads=[rbt], writes=["EB"])
        sc.op("act", lambda e, pbt=pbt: e.activation(out=ENB[:], in_=pbt[:, :], func=AF.Exp, scale=-1.0), reads=[rbt], writes=["ENB"])
        pd_, rd_ = gbank()
        def d_mm(e, pd_=pd_):
            ins = None
            for c in range(4):
                ins = e.matmul(pd_[:, c * 128:(c + 1) * 128], lhsT=texc[:], rhs=SP[:, c * 128:(c + 1) * 128], start=True, stop=True)
            return ins
        sc.op("pe", d_mm, reads=["SP", "texc"], writes=[rd_])
        sc.op("act", lambda e, pd_=pd_: e.activation(out=SP[:], in_=pd_[:, :], func=AF.Exp), reads=[rd_], writes=["SP"])
        sc.op("pool", lambda e: e.tensor_tensor(out=kdec[:].rearrange("p a b -> p (a b)"), in0=gktok[:].rearrange("p a b -> p (a b)"), in1=SP[:], op=ALU.mult),
              reads=[("gktok", i) for i in range(4)] + ["SP"], writes=["kdec"])
        yield
        p_, r_ = fproj("gq")
        sc.op("dve", lambda e, p_=p_: e.scalar_tensor_tensor(out=gqt[:], in0=p_[:, :], scalar=float(128 ** -0.5), in1=EB[:], op0=ALU.mult, op1=ALU.mult),
              reads=[r_, "EB"], writes=["gqt"])
        yield
        p_, r_ = fproj("gk")
        sc.op("dve", lambda e, p_=p_: e.tensor_tensor(out=gkt[:], in0=p_[:, :], in1=ENB[:], op=ALU.mult),
              reads=[r_, "ENB"], writes=["gkt"])
        sc.stage = 0
        if t + 1 < NT:
            issue_x(t + 1)
        yield

        sc.stage = 7
        for mb in range(2):
            pq_, rq_ = gbank()
            sc.op("pe", lambda e, pq_=pq_, mb=mb: e.matmul(pq_[:, :], lhsT=mkT[:, mb * 128:(mb + 1) * 128], rhs=mqT[:], start=True, stop=True),
                  reads=["mkT", "mqT"], writes=[rq_])
            sc.op("act", lambda e, pq_=pq_, mb=mb: e.activation(out=pm[:, mb, :], in_=pq_[:, :], func=AF.Exp, scale=float(128 ** -0.5)),
                  reads=[rq_], writes=[("pm", mb)])
        yield
        for half in range(2):
            po_, ro_ = gbank()
            def mo_mm(e, po_=po_, half=half):
                ins = None
                for i in range(2):
                    tb = half * 2 + i
                    for mb in range(2):
                        ins = e.matmul(po_[:, i * 144:i * 144 + 129], lhsT=pm[:, mb, tb * 128:(tb + 1) * 128], rhs=mv1[:, mb, :], start=(mb == 0), stop=(mb == 1))
                return ins
            sc.op("pe", mo_mm, reads=[("pm", 0), ("pm", 1), "mv1", "mv1ones"], writes=[ro_])
            for i in range(2):
                tb = half * 2 + i
                sc.op("dve", lambda e, po_=po_, i=i: e.reciprocal(out=sml[:, 4:5], in_=po_[:, i * 144 + 128:i * 144 + 129]),
                      reads=[ro_], writes=["sml4"])
                sc.op("dve", lambda e, po_=po_, i=i, tb=tb: e.scalar_tensor_tensor(out=mix[:, tb, 384:512], in0=po_[:, i * 144:i * 144 + 128], scalar=sml[:, 4:5], in1=mgs[:, tb, :], op0=ALU.mult, op1=ALU.mult),
                      reads=[ro_, "sml4", ("mgs", tb)], writes=[("mix", pb, tb, "m")])
            yield

        sc.stage = 6
        for c in range(4):
            cs = slice(c * 128, (c + 1) * 128)
            px, rx = gbank()
            py, ry = gbank()
            sc.op("pe", lambda e, px=px, cs=cs: e.matmul(px[:, 256:384], lhsT=gkt[:, cs], rhs=gqt[:, cs], start=True, stop=True),
                  reads=["gkt", "gqt"], writes=[rx])
            sc.op("dve", lambda e, px=px: e.tensor_tensor(out=sTm[:], in0=px[:, 256:384], in1=tri[:], op=ALU.mult),
                  reads=[rx, "tri"], writes=["sTm"])
            sc.op("pe", lambda e, py=py, c=c: e.matmul(py[:, 0:256], lhsT=kdec[:, c, :], rhs=gvb[:, c, :], start=True, stop=True),
                  reads=["kdec", ("gvb", c)], writes=[ry])
            yield
            sc.op("pe", lambda e, px=px, cs=cs: e.matmul(px[:, 0:256], lhsT=gqt[:, cs], rhs=Sbf[:], start=True, stop=False),
                  reads=["gqt", "Sbf"], writes=[rx])
            sc.op("pe", lambda e, px=px, c=c: e.matmul(px[:, 0:256], lhsT=sTm[:], rhs=gvb[:, c, :], start=False, stop=True),
                  reads=["sTm", ("gvb", c)], writes=[rx])
            sc.op("dve", lambda e, py=py, c=c: e.scalar_tensor_tensor(out=Sst[:], in0=Sst[:], scalar=EB[:, c * 128 + 127:c * 128 + 128], in1=py[:, 0:256], op0=ALU.mult, op1=ALU.add),
                  reads=["S", "EB", ry], writes=["S"])
            sc.op("pool", lambda e: e.tensor_copy(out=Sbf[:], in_=Sst[:]), reads=["S"], writes=["Sbf"])
            sc.op("act", lambda e, px=px: e.activation(out=junk[:], in_=px[:, 0:256], func=AF.Square, accum_out=sml[:, 0:1]),
                  reads=[rx], writes=["junk", "sml0"])
            sc.op("act", lambda e: e.activation(out=sml[:, 1:2], in_=sml[:, 0:1], func=AF.Ln, scale=float(1.0 / 256), bias=col(CI_EPSG)),
                  reads=["sml0", "cst"], writes=["sml1"])
            sc.op("act", lambda e: e.activation(out=sml[:, 2:3], in_=sml[:, 1:2], func=AF.Exp, scale=-0.5),
                  reads=["sml1"], writes=["sml2"])
            sc.op("dve", lambda e, px=px, c=c: e.scalar_tensor_tensor(out=mix[:, c, 0:256], in0=px[:, 0:256], scalar=sml[:, 2:3], in1=gsil[:, c, :], op0=ALU.mult, op1=ALU.mult),
                  reads=[rx, "sml2", ("gsil", c)], writes=[("mix", pb, c, "g")])
            yield

    def diff_steps(t):
        sc.stage = 8
        pb = t % 2
        qT2 = qT2s[pb]
        started = set()

        def issue_s(hq, kb):
            d = kb - (4 * t + 2 * hq)
            lo = 128 if d == 1 else 0
            n = 256 - lo
            ps = PS[psc[0] % 2]
            rps = ("PS", psc[0] % 2)
            psc[0] += 1
            pt = pT[ptc[0] % NPT]
            rpt = ("pT", ptc[0] % NPT)
            ptc[0] += 1
            psv = ps[:, :]
            ptv = pt[:, :]
            q0 = hq * 256
            sc.op("pe", lambda e, psv=psv, kb=kb, q0=q0: e.matmul(
                psv, lhsT=KT[:, kb * 128:(kb + 1) * 128], rhs=qT2[:, :, q0:q0 + 256], start=True, stop=True),
                reads=[("KT", kb // 4), ("qT2", pb)], writes=[rps])
            sc.op("act", lambda e, psv=psv, ptv=ptv: e.activation(out=ptv, in_=psv, func=AF.Exp, scale=0.125),
                  reads=[rps], writes=[rpt])
            if d >= 0:
                for c in range(2):
                    sc.op("pool", lambda e, pt=pt, c=c, lo=lo: e.tensor_tensor(
                        out=pt[:, c * 256 + lo:c * 256 + lo + 128], in0=pt[:, c * 256 + lo:c * 256 + lo + 128], in1=tri[:], op=ALU.mult),
                        reads=[rpt, "tri"], writes=[rpt])
            return pt, rpt, lo

        def issue_pv(hq, kb, pt, rpt, lo):
            wr = set()
            plan = []
            for c in range(2):
                for ql in range(lo // 128, 2):
                    qb = 2 * hq + ql
                    reg, rk_ = pa_region(c, qb)
                    first = rk_ not in started
                    started.add(rk_)
                    wr.add(rk_)
                    plan.append((reg, c * 256 + ql * 128, first, kb == 4 * t + qb))
            def pv(e, plan=plan, pt=pt, kb=kb):
                ins = None
                for reg, off_, first, last in plan:
                    ins = e.matmul(reg, lhsT=pt[:, off_:off_ + 128], rhs=V1[:, kb, :], start=first, stop=last, skip_group_check=True)
                return ins
            sc.op("pe", pv, reads=[rpt, ("V1", kb), "V1ones"], writes=sorted(wr))

        steps = [(hq, kb) for hq in range(2) for kb in range(4 * t + 2 * hq + 2)]
        pend = issue_s(*steps[0])
        for si in range(len(steps)):
            nxt = issue_s(*steps[si + 1]) if si + 1 < len(steps) else None
            issue_pv(steps[si][0], steps[si][1], *pend)
            pend = nxt
            yield

    def epilogue(t):
        sc.stage = 8
        pb = t % 2
        mix = mixs[pb]
        dgs = dgss[pb]
        for qb in range(4):
            r1, rr1 = pa_region(0, qb)
            r2, rr2 = pa_region(1, qb)
            sc.op("dve", lambda e, r1=r1: e.reciprocal(out=sml[:, 6:7], in_=r1[:, 128:129]), reads=[rr1], writes=["sml6"])
            sc.op("dve", lambda e, r2=r2: e.reciprocal(out=sml[:, 7:8], in_=r2[:, 128:129]), reads=[rr2], writes=["sml7"])
            sc.op("dve", lambda e: e.tensor_tensor(out=sml[:, 8:9], in0=sml[:, 7:8], in1=lam[:], op=ALU.mult), reads=["sml7", "lam"], writes=["sml8"])
            sc.op("dve", lambda e, r2=r2: e.tensor_scalar(out=tmpd[:], in0=r2[:, 0:128], scalar1=sml[:, 8:9], scalar2=None, op0=ALU.mult),
                  reads=[rr2, "sml8"], writes=["tmpd"])
            sc.op("dve", lambda e, r1=r1: e.scalar_tensor_tensor(out=dif[:], in0=r1[:, 0:128], scalar=sml[:, 6:7], in1=tmpd[:], op0=ALU.mult, op1=ALU.subtract),
                  reads=[rr1, "sml6", "tmpd"], writes=["dif"])
            sc.op("act", lambda e: e.activation(out=junk2[:], in_=dif[:], func=AF.Square, accum_out=sml[:, 9:10]),
                  reads=["dif"], writes=["junk2", "sml9"])
            sc.op("act", lambda e: e.activation(out=sml[:, 10:11], in_=sml[:, 9:10], func=AF.Ln, scale=float(1.0 / 128), bias=col(CI_EPSD)),
                  reads=["sml9", "cst"], writes=["sml10"])
            sc.op("act", lambda e: e.activation(out=sml[:, 11:12], in_=sml[:, 10:11], func=AF.Exp, scale=-0.5),
                  reads=["sml10"], writes=["sml11"])
            sc.op("dve", lambda e, qb=qb: e.scalar_tensor_tensor(out=mix[:, qb, 256:384], in0=dif[:], scalar=sml[:, 11:12], in1=dgs[:, qb, :], op0=ALU.mult, op1=ALU.mult),
                  reads=["dif", "sml11", ("dgs", pb, qb)], writes=[("mix", pb, qb, "d")])
        sc.stage = 9
        mixres = [("mix", pb, tb, k) for tb in range(4) for k in ("g", "d", "m")]
        mix_stores.append(sc.dma("sp", lambda e, t=t: e.dma_start(out=mixv[:, 4 * t:4 * t + 4, :], in_=mix[:]), reads=mixres))

    sc.stage = 0
    issue_x(0)
    issue_pos(0)
    for _ in front(0):
        pass
    for t in range(NT):
        g = front(t + 1) if t + 1 < NT else None
        nsteps = 8 * t + 6
        NFRONT = 24
        done = 0
        for si, _ in enumerate(diff_steps(t)):
            if g is not None:
                want = ((si + 1) * NFRONT + nsteps - 1) // nsteps
                while done < want:
                    try:
                        next(g)
                    except StopIteration:
                        g = None
                        break
                    done += 1
        if g is not None:
            for _ in g:
                pass
        epilogue(t)

    sc.stage = 10
    cc = sc.collective(lambda e: e.collective_compute(
        "AllGather", mybir.AluOpType.bypass, replica_groups=[list(range(8))],
        ins=[mixbuf.ap().opt()], outs=[gath.ap().opt()]), reads=[], writes=["gath"])
    cc.waits.extend(o for o in mix_stores if o.token is not None)

    sc.stage = 11
    allkv = [("KT", i) for i in range(NT)] + [("V1", i) for i in range(S // 128)] + ["V1ones"]
    woutv = wout.rearrange("(k p) n -> p k n", p=128)
    for j in range(4):
        sc.dma("pool", lambda e, j=j: e.dma_start(out=WOUT[:, 4 * j:4 * j + 4, :], in_=woutv[:, 4 * j:4 * j + 4, :]),
               writes=[("WOUT", j)] + (allkv if j == 0 else []))
    WOres = [("WOUT", j) for j in range(4)]
    a2 = WTres + WFres
    sc.dma("sp", lambda e: e.dma_start(out=lng, in_=lng_d), writes=["lng"] + a2)
    sc.dma("sp", lambda e: e.dma_start(out=lnb, in_=lnb_d), writes=["lnb"])
    pidc = {}

    def pid_of(e, name):
        if name not in pidc:
            pidc[name] = e.partition_id()
        return pidc[name]

    gv = gath.ap()
    sel = nc.dram_tensor("sel", [4 * SQ, 512], BF16)
    selv = sel.ap().rearrange("(h s) c -> s h c", h=4)
    for h in range(4):
        def ldsel(e, h=h):
            pid = e.partition_id()
            row = ((pid // 4) * 4 + h) * S + (pid % 4) * SQ
            return e.dma_start(out=sel.ap()[h * SQ:(h + 1) * SQ, :], in_=gv[bass.ds(row, SQ), :])
        sc.dma("sp", ldsel, reads=["gath"], writes=[("sel", h)])
    alpha = float((2.0 * 1) ** 0.25)
    out_stores = []
    for tb in range(SQ // 128):
        bi = tb % 2
        sc.stage = 11.1
        sc.dma("pool", lambda e, tb=tb, bi=bi: e.dma_start(out=mrow[bi], in_=selv[tb * 128:(tb + 1) * 128, :, :]),
               reads=[("sel", h) for h in range(4)], writes=[("mrow", bi, h) for h in range(4)])
        sc.dma("sp", lambda e, tb=tb, bi=bi: e.dma_start(out=xrow[bi], in_=xq[tb * 128:(tb + 1) * 128, :]), writes=[("xrow", bi)] + [("yrow", bi, cb) for cb in range(4)])
        sc.stage = 11.2
        for h in range(4):
            ptb, rtb = gbank()
            def tr(e, h=h, bi=bi, ptb=ptb):
                ins = None
                for j in range(4):
                    ins = e.transpose(ptb[:, j * 128:(j + 1) * 128], mrow[bi][:, h, j * 128:(j + 1) * 128], ident[:])
                return ins
            sc.op("pe", tr, reads=[("mrow", bi, hh) for hh in range(4)] + ["ident"], writes=[rtb])
            sc.op("act", lambda e, h=h, bi=bi, ptb=ptb: e.activation(
                out=mixT[bi][:, h * 4:h * 4 + 4, :], in_=ptb[:, :].rearrange("p (k c) -> p k c", c=128), func=AF.Copy),
                reads=[rtb], writes=[("mixT", bi, h)])
        sc.stage = 11.3
        for cb in range(4):
            po_, ro_ = gbank()
            def omm(e, po_=po_, cb=cb, bi=bi):
                ins = None
                for fc in range(16):
                    ins = e.matmul(po_[:, :], lhsT=mixT[bi][:, fc, :], rhs=WOUT[:, fc, cb * 512:(cb + 1) * 512], start=(fc == 0), stop=(fc == 15))
                return ins
            sc.op("pe", omm, reads=[("mixT", bi, hh) for hh in range(4)] + WOres, writes=[ro_])
            sc.op("dve", lambda e, po_=po_, cb=cb, bi=bi: e.scalar_tensor_tensor(
                out=yrow[bi][:, cb * 512:(cb + 1) * 512], in0=xrow[bi][:, cb * 512:(cb + 1) * 512], scalar=alpha, in1=po_[:, :], op0=ALU.mult, op1=ALU.add),
                reads=[ro_, ("xrow", bi)], writes=[("yrow", bi, cb), ("xrow", bi)])
            sc.op("dve", lambda e, cb=cb, bi=bi: e.bn_stats(out=bnst[:, cb * 6:cb * 6 + 6], in_=yrow[bi][:, cb * 512:(cb + 1) * 512]),
                  reads=[("yrow", bi, cb)], writes=[("bnst", cb)])
        sc.stage = 11.4
        yres = [("yrow", bi, cb) for cb in range(4)]
        sc.op("dve", lambda e: e.bn_aggr(out=bnst[:, 24:26], in_=bnst[:, 0:24]), reads=[("bnst", cb) for cb in range(4)], writes=["mv"])
        sc.op("act", lambda e: e.activation(out=bnst[:, 26:27], in_=bnst[:, 25:26], func=AF.Ln, bias=col(CI_EPSL)), reads=["mv", "cst"], writes=["lnv"])
        sc.op("act", lambda e: e.activation(out=bnst[:, 27:28], in_=bnst[:, 26:27], func=AF.Exp, scale=-0.5), reads=["lnv"], writes=["rstd"])
        sc.op("dve", lambda e, bi=bi: e.tensor_scalar(out=yrow[bi], in0=yrow[bi], scalar1=bnst[:, 24:25], scalar2=bnst[:, 27:28], op0=ALU.subtract, op1=ALU.mult),
              reads=yres + ["mv", "rstd"], writes=yres)
        sc.op("pool", lambda e, bi=bi: e.tensor_tensor(out=yrow[bi], in0=yrow[bi], in1=lng, op=ALU.mult), reads=yres + ["lng"], writes=yres)
        sc.op("pool", lambda e, bi=bi: e.tensor_tensor(out=yrow[bi], in0=yrow[bi], in1=lnb, op=ALU.add), reads=yres + ["lnb"], writes=yres)
        out_stores.append(sc.dma("sp", lambda e, tb=tb, bi=bi: e.dma_start(out=y[tb * 128:(tb + 1) * 128, :], in_=yrow[bi]), reads=yres))

    sc.fence("sp", [o for o in out_stores if o.token is not None])
    with nc.allow_low_precision("bf16 matmul operands, fp32 accumulation"):
        sc.emit(nc, es)
    es.close()
    return nc


def _prep_inputs(x, mem, positions, w_in, w_gk_up, b_gk_up, gla_norm_g, lambda_q1, lambda_k1,
                 lambda_q2, lambda_k2, diff_norm_g, w_mem_kv, w_out, ln_g, ln_b):
    f32 = np.float32
    x = np.asarray(x, f32)
    mem = np.asarray(mem, f32)
    positions = np.asarray(positions, np.int32)
    w_in = np.asarray(w_in, f32)[0]
    w_up = np.asarray(w_gk_up, f32)[0]
    b_up = np.asarray(b_gk_up, f32)[0]
    w_mem = np.asarray(w_mem_kv, f32)[0]
    w_o = np.asarray(w_out, f32)[0]

    def rep(v, n=128):
        return np.ascontiguousarray(np.broadcast_to(np.asarray(v, f32).reshape(1, -1), (n, np.asarray(v).size)))

    half = (np.arange(0, 16, 2, dtype=f32) / f32(16)).astype(f32)
    inv_freq = np.power(f32(500000.0), -half).astype(f32)
    cst = np.zeros((128, 8), f32)
    for p in range(128):
        d = p % 64
        if d < 16:
            cst[p, CI_INVF] = inv_freq[d % 8]
            cst[p, CI_SGN] = -1.0 if d < 8 else 1.0
            cst[p, CI_NOTROT] = -2.0
        else:
            cst[p, CI_NOTROT] = 1.0
    cst[:, CI_HALFPI] = math.pi / 2
    cst[:, CI_ONE] = 1.0
    cst[:, CI_EPSG] = 1e-6
    cst[:, CI_EPSD] = 1e-5
    cst[:, CI_EPSL] = 1e-5
    jj, ii = np.meshgrid(np.arange(128), np.arange(128), indexing="ij")
    tri = (jj <= ii).astype(f32)
    tinc = tri * f32(-1.0 / 16)
    texc = (jj > ii).astype(f32) * f32(-1.0 / 16)
    ident = np.eye(128, dtype=f32)
    swap = np.arange(128)
    for p in range(128):
        d = p % 64
        if d < 8:
            swap[p] = p + 8
        elif d < 16:
            swap[p] = p - 8

    rows = []
    for h in range(4):
        rows.append(np.arange(h * 256, h * 256 + 256))
        rows.append(np.arange(1024 + h * 128, 1024 + h * 128 + 128))
        rows.append(np.arange(1536 + h * 128, 1536 + h * 128 + 128))
    woutP = np.ascontiguousarray(w_o[np.concatenate(rows)])
    lamp = rep(np.concatenate([np.asarray(a, f32).reshape(-1) for a in (lambda_q1, lambda_k1, lambda_q2, lambda_k2)]))
    glag = rep(np.asarray(gla_norm_g, f32).reshape(-1))
    dgg = rep(np.asarray(diff_norm_g, f32).reshape(-1))
    lng = rep(np.asarray(ln_g, f32).reshape(-1))
    lnb = rep(np.asarray(ln_b, f32).reshape(-1))

    xTs = [np.ascontiguousarray(x[b].T) for b in range(2)]
    memTs = [np.ascontiguousarray(mem[b].T) for b in range(2)]
    posrs = [np.ascontiguousarray(np.broadcast_to(positions[b][None, :], (128, S))) for b in range(2)]
    in_maps = []
    for c in range(8):
        b, h = c // 4, c % 4
        gq = w_in[:, 0 + h * 128:0 + h * 128 + 128]
        gk = w_in[:, 512 + h * 128:512 + h * 128 + 128]
        gv = w_in[:, 1024 + h * 256:1024 + h * 256 + 256]
        gg = w_in[:, 2048 + h * 256:2048 + h * 256 + 256]
        lr = w_in[:, 3072:3088]
        dq = w_in[:, 3088 + h * 128:3088 + h * 128 + 128]
        dk = w_in[:, 3600 + h * 128:3600 + h * 128 + 128]
        dv = w_in[:, 4112 + h * 128:4112 + h * 128 + 128]
        dg = w_in[:, 4624 + h * 128:4624 + h * 128 + 128]
        mq = w_in[:, 5136 + h * 128:5136 + h * 128 + 128]
        mg = w_in[:, 5648 + h * 128:5648 + h * 128 + 128]
        wF = np.ascontiguousarray(np.concatenate([gq, gk, dq, dk, dq[:, swap], dk[:, swap], mq, lr], axis=1))
        wT = np.ascontiguousarray(np.concatenate([gv, gg, gk, dv, dg, mg], axis=1))
        wm = np.ascontiguousarray(np.concatenate([w_mem[:, h * 128:h * 128 + 128], w_mem[:, 512 + h * 128:512 + h * 128 + 128]], axis=1))
        wup = np.zeros((33, 128), f32)
        wup[0:16] = w_up[:, h * 128:h * 128 + 128]
        wup[32] = b_up[h * 128:h * 128 + 128]
        in_maps.append({
            "xT": xTs[b], "xq": np.ascontiguousarray(x[b, h * SQ:(h + 1) * SQ, :]),
            "wF": wF, "wT": wT, "memT": memTs[b], "wm": wm, "wup": wup, "wout": woutP,
            "posr": posrs[b], "glag": glag, "dgg": dgg, "lamp": lamp, "lng": lng, "lnb": lnb,
            "cst": cst, "tri": tri, "tinc": tinc, "texc": texc, "ident": ident,
        })
    return in_maps


_NC_CACHE = {}


def _set_S(s_):
    global S, NT, SQ
    S = s_
    NT = S // 512
    SQ = S // 4


def kernel(**inputs):
    in_maps = _prep_inputs(**inputs)
    if "nc" not in _NC_CACHE:
        _NC_CACHE["nc"] = build_nc()
    nc = _NC_CACHE["nc"]
    res = run_bass_kernel_spmd(nc, in_maps, core_ids=list(range(8)))
    out = np.empty((2, S, D), np.float32)
    for c in range(8):
        b, h = c // 4, c % 4
        out[b, h * SQ:(h + 1) * SQ, :] = res.results[c]["y"]
    return out
```

```python
import math
from contextlib import ExitStack

import numpy as np
import concourse.bass as bass
import concourse.mybir as mybir
from concourse.bass_utils import run_bass_kernel_spmd

F32 = mybir.dt.float32
BF16 = mybir.dt.bfloat16
I32 = mybir.dt.int32
AF = mybir.ActivationFunctionType
ALU = mybir.AluOpType

D = 2048
S = 16384
NT = S // 512
NKC = D // 128
SQ = S // 4
WF_N = 7 * 128 + 16
WT_N = 1024
LAM_INIT = 0.8 - 0.6 * math.exp(-0.3 * 0)
TWO_PI = 2.0 * math.pi
C1 = 6.28125
C2 = TWO_PI - C1

FB = {"gq": 0, "gk": 128, "dq": 256, "dk": 384, "dqs": 512, "dks": 640, "mq": 768, "lr": 896}

CI_INVF, CI_SGN, CI_NOTROT, CI_HALFPI, CI_ONE, CI_EPSG, CI_EPSD, CI_EPSL = range(8)


class Op:
    __slots__ = ("eng", "fn", "waits", "signal", "token", "inc", "sigidx")

    def __init__(self, eng, fn):
        self.eng = eng
        self.fn = fn
        self.waits = []
        self.signal = False
        self.token = None
        self.inc = 1
        self.sigidx = None


class Sched:
    ENGS = ["pe", "act", "dve", "pool", "sp"]
    NSLOT = 8

    def __init__(self):
        self.ops = {e: [] for e in self.ENGS}
        self.last_writer = {}
        self.readers = {}
        self.slot_prev = {}
        self.slot_uses = {}
        self.next_slot = {"sp": 0, "pool": 0}
        self.dma_sems = {}
        self.eng_sems = {}
        self.cc_sem = None

    @staticmethod
    def _is_psum(r):
        return isinstance(r, tuple) and r[0] in ("PG", "PS", "PAB")

    def _deps(self, op, reads, writes):
        writes = list(writes) + [r for r in reads if self._is_psum(r)]
        reads = [r for r in reads if not self._is_psum(r)]
        deps = []
        for r in reads:
            w = self.last_writer.get(r)
            if w is not None:
                deps.append(w)
        for w_ in writes:
            w = self.last_writer.get(w_)
            if w is not None:
                deps.append(w)
            deps.extend(self.readers.get(w_, ()))
        for w_ in writes:
            self.last_writer[w_] = op
            self.readers[w_] = []
        for r in reads:
            lst = self.readers.setdefault(r, [])
            if op.token is None:
                lst[:] = [o for o in lst if not (o.eng == op.eng and o.token is None)]
            lst.append(op)
        seen = set()
        for d in deps:
            if d is op or id(d) in seen:
                continue
            seen.add(id(d))
            if d.eng == "pe" and op.eng == "pe" and d.token is None and op.token is None:
                continue
            if d.token is None:
                d.signal = True
            op.waits.append(d)

    stage = 0
    level = 99

    def op(self, eng, fn, reads=(), writes=()):
        o = Op(eng, fn)
        if self.stage > self.level:
            return o
        self._deps(o, reads, writes)
        self.ops[eng].append(o)
        return o

    def dma(self, q, fn, reads=(), writes=()):
        o = Op(q, fn)
        if self.stage > self.level:
            return o
        slot = self.next_slot[q] % self.NSLOT
        self.next_slot[q] += 1
        key = (q, slot)
        uses = self.slot_uses.get(key, 0) + 1
        self.slot_uses[key] = uses
        o.token = (("dma", q, slot), 16 * uses)
        o.inc = 16
        prev = self.slot_prev.get(key)
        if prev is not None:
            o.waits.append(prev)
        self.slot_prev[key] = o
        self._deps(o, reads, writes)
        self.ops[q].append(o)
        return o

    def collective(self, fn, reads=(), writes=()):
        o = Op("pool", fn)
        if self.stage > self.level:
            return o
        o.token = (("cc",), 1)
        o.inc = 1
        self._deps(o, reads, writes)
        self.ops["pool"].append(o)
        return o

    def fence(self, eng, deps):
        o = Op(eng, None)
        for d in deps:
            if d.token is None and d.sigidx is None and not any(d is x for x in self.ops[d.eng]):
                continue
            if d.token is None:
                d.signal = True
            o.waits.append(d)
        self.ops[eng].append(o)
        return o

    def emit(self, nc, es):
        for e in self.ENGS:
            self.eng_sems[e] = es.enter_context(nc.semaphore("sem_" + e))
        for q in ("sp", "pool"):
            for s in range(self.NSLOT):
                self.dma_sems[("dma", q, s)] = es.enter_context(nc.semaphore("dsem_%s_%d" % (q, s)))
        self.cc_sem = es.enter_context(nc.semaphore("ccsem"))
        for e in self.ENGS:
            n = 0
            for o in self.ops[e]:
                if o.token is None and o.signal:
                    n += 1
                    o.sigidx = n
        block = es.enter_context(nc.Block())

        def tok(o):
            if o.token is None:
                return ("eng", o.eng), o.sigidx
            return o.token

        def semof(k):
            if k[0] == "eng":
                return self.eng_sems[k[1]]
            if k[0] == "cc":
                return self.cc_sem
            return self.dma_sems[k]

        def run(e):
            def body(eng):
                known = {}
                for o in self.ops[e]:
                    for d in o.waits:
                        k, v = tok(d)
                        if known.get(k, 0) < v:
                            eng.wait_ge(semof(k), v)
                            known[k] = v
                    if o.fn is None:
                        continue
                    ins = o.fn(eng)
                    if o.token is not None:
                        if o.token[0][0] == "cc":
                            ins.then_inc(self.cc_sem)
                        else:
                            ins.then_inc(self.dma_sems[o.token[0]], 16)
                    elif o.signal:
                        ins.then_inc(self.eng_sems[e], 1)
            return body

        block.tensor(run("pe"))
        block.scalar(run("act"))
        block.vector(run("dve"))
        block.gpsimd(run("pool"))
        block.sync(run("sp"))


def build_nc(level=99):
    nc = bass.Bass("TRN2", target_bir_lowering=False)
    es = ExitStack()
    sc = Sched()
    sc.level = level

    def dram_in(name, shape, dt=F32):
        return nc.dram_tensor(name, list(shape), dt, kind="ExternalInput").ap()

    xT = dram_in("xT", [D, S])
    xq = dram_in("xq", [SQ, D])
    wF = dram_in("wF", [D, WF_N])
    wT = dram_in("wT", [D, WT_N])
    memT = dram_in("memT", [D, 256])
    wm = dram_in("wm", [D, 256])
    wup = dram_in("wup", [33, 128])
    wout = dram_in("wout", [D, D])
    posr = dram_in("posr", [128, S], I32)
    glag_d = dram_in("glag", [128, 256])
    dgg_d = dram_in("dgg", [128, 128])
    lamp_d = dram_in("lamp", [128, 256])
    lng_d = dram_in("lng", [128, D])
    lnb_d = dram_in("lnb", [128, D])
    cst_d = dram_in("cst", [128, 8])
    tri_d = dram_in("tri", [128, 128])
    tinc_d = dram_in("tinc", [128, 128])
    texc_d = dram_in("texc", [128, 128])
    ident_d = dram_in("ident", [128, 128])
    y = nc.dram_tensor("y", [SQ, D], F32, kind="ExternalOutput").ap()
    mixbuf = nc.dram_tensor("mixbuf", [S, 512], BF16)
    gath = nc.dram_tensor("gath", [8 * S, 512], BF16)

    def sb(name, shape, dt):
        return es.enter_context(nc.sbuf_tensor("s_" + name, list(shape), dt))

    arena = sb("arena", [128, 16384 + 128 * 129], BF16)
    KT = arena[:, 0:S]
    V1 = arena[:, 16384:16384 + (S // 128) * 129].rearrange("p (k c) -> p k c", c=129)
    WOUT = arena[:, 0:32768].rearrange("p (k n) -> p k n", n=2048)
    arena2 = sb("arena2", [128, NKC * (WF_N + WT_N)], BF16)
    WF = arena2[:, 0:NKC * WF_N].rearrange("p (k n) -> p k n", n=WF_N)
    WT = arena2[:, NKC * WF_N:NKC * (WF_N + WT_N)].rearrange("p (k n) -> p k n", n=WT_N)
    xb = sb("xb", [128, NKC, 512], BF16)
    memTb = xb[:, :, 0:256]
    wmb = xb[:, :, 256:512]
    mkT = sb("mkT", [128, 256], BF16)
    mv1 = sb("mv1", [128, 2, 129], BF16)
    wup_s = sb("wup_s", [33, 128], F32)
    cst = sb("cst", [128, 8], F32)
    tri = sb("tri", [128, 128], BF16)
    tinc = sb("tinc", [128, 128], F32)
    texc = sb("texc", [128, 128], F32)
    ident = sb("ident", [128, 128], F32)
    glag = sb("glag_s", [128, 256], F32)
    dgg = sb("dgg_s", [128, 128], F32)
    lamp = sb("lamp_s", [128, 256], F32)
    lam = sb("lam", [128, 1], F32)
    lamt = sb("lamt", [128, 4], F32)
    Sst = sb("Sst", [128, 256], F32)
    Sbf = sb("Sbf", [128, 256], BF16)
    bufK = sb("bufK", [128, 512], I32)
    posi = bufK
    bufR = sb("bufR", [128, 512], F32)
    sinS = sb("sinS", [128, 512], F32)
    cosT = sb("cosT", [128, 512], F32)
    T1 = sb("T1", [128, 512], F32)
    T2 = sb("T2", [128, 512], F32)
    qT2s = [sb("qT2_%d" % i, [128, 2, 512], BF16) for i in range(2)]
    gqt = sb("gqt", [128, 512], BF16)
    gkt = sb("gkt", [128, 512], BF16)
    mqT = sb("mqT", [128, 512], BF16)
    lrT = sb("lrT", [33, 512], F32)
    SP = sb("SP", [128, 512], F32)
    EB = sb("EB", [128, 512], F32)
    ENB = sb("ENB", [128, 512], F32)
    gktok = sb("gktok", [128, 4, 128], F32)
    kdec = sb("kdec", [128, 4, 128], BF16)
    gvb = sb("gvb", [128, 4, 256], BF16)
    gsil = sb("gsil", [128, 4, 256], F32)
    dgss = [sb("dgs%d" % i, [128, 4, 128], F32) for i in range(2)]
    mgs = sb("mgs", [128, 4, 128], F32)
    sTm = sb("sTm", [128, 128], BF16)
    junk = sb("junk", [128, 256], F32)
    junk2 = sb("junk2", [128, 128], F32)
    sml = sb("sml", [128, 16], F32)
    pm = sb("pm", [128, 2, 512], BF16)
    NPT = 4
    pT = [sb("pT%d" % i, [128, 512], BF16) for i in range(NPT)]
    tmpd = sb("tmpd", [128, 128], F32)
    dif = sb("dif", [128, 128], F32)
    mixs = [sb("mix%d" % i, [128, 4, 512], BF16) for i in range(2)]

    off = [0]

    def carve(nelem_bf16):
        a = arena2[:, off[0]:off[0] + nelem_bf16]
        off[0] += nelem_bf16
        return a

    mrow = [carve(2 * 4 * 512).bitcast(F32).rearrange("p (h c) -> p h c", c=512) for _ in range(2)]
    mixT = [carve(16 * 128).rearrange("p (k c) -> p k c", c=128) for _ in range(2)]
    xrow = [carve(2 * D).bitcast(F32) for _ in range(2)]
    yrow = xrow
    lng = carve(2 * D).bitcast(F32)
    lnb = carve(2 * D).bitcast(F32)
    bnst = carve(2 * 32).bitcast(F32)
    assert off[0] <= NKC * (WF_N + WT_N)

    PB = [es.enter_context(nc.psum_tensor("pb%d" % i, [128, 512], F32)) for i in range(8)]
    PG = PB[0:3]
    PS = PB[3:5]
    PA = PB[5:8]
    gctr = [0]

    def gbank():
        i = gctr[0] % 3
        gctr[0] += 1
        return PG[i], ("PG", i)

    def pa_region(c, qb):
        idx = c * 4 + qb
        return PA[idx // 3][:, (idx % 3) * 144:(idx % 3) * 144 + 129], ("PAB", idx // 3)

    def col(i):
        return cst[:, i:i + 1]

    def ld_sp(dst, src, res):
        return sc.dma("sp", lambda e, d=dst, s=src: e.dma_start(out=d, in_=s), writes=[res])

    def ld_cast(dst, src, res):
        return sc.dma("pool", lambda e, d=dst, s=src: e.dma_start(out=d, in_=s), writes=[res])

    ld_sp(cst[:], cst_d, "cst")
    ld_sp(tinc[:], tinc_d, "tinc")
    ld_sp(texc[:], texc_d, "texc")
    ld_sp(wup_s[:], wup, "wup")
    ld_sp(glag[:], glag_d, "glag")
    ld_sp(dgg[:], dgg_d, "dgg")
    ld_sp(lamp[:], lamp_d, "lamp")
    ld_cast(tri[:], tri_d, "tri")
    ld_sp(ident[:], ident_d, "ident")
    XB0 = [("xb", j) for j in range(4)]
    sc.dma("pool", lambda e: e.dma_start(out=memTb, in_=memT.rearrange("(k p) m -> p k m", p=128)), writes=XB0)
    sc.dma("pool", lambda e: e.dma_start(out=wmb, in_=wm.rearrange("(k p) m -> p k m", p=128)), writes=XB0)
    wTv = wT.rearrange("(k p) n -> p k n", p=128)
    wFv = wF.rearrange("(k p) n -> p k n", p=128)
    for j in range(4):
        ld_cast(WT[:, 4 * j:4 * j + 4, :], wTv[:, 4 * j:4 * j + 4, :], ("WT", j))
    for j in range(4):
        ld_cast(WF[:, 4 * j:4 * j + 4, :], wFv[:, 4 * j:4 * j + 4, :], ("WF", j))
    WTres = [("WT", j) for j in range(4)]
    WFres = [("WF", j) for j in range(4)]

    sc.op("pool", lambda e: e.tensor_scalar(out=dgg[:], in0=dgg[:], scalar1=float(1.0 - LAM_INIT), scalar2=None, op0=ALU.mult),
          reads=["dgg"], writes=["dgg"])
    for i in range(2):
        sc.op("pool", lambda e, i=i: e.memset(qT2s[i][:], 0.0), writes=[("qT2", i)])
    sc.op("pool", lambda e: e.memset(Sst[:], 0.0), writes=["S"])
    sc.op("pool", lambda e: e.memset(Sbf[:], 0.0), writes=["Sbf"])
    sc.op("pool", lambda e: e.memset(lrT[:], 0.0), writes=["lrT"])
    sc.op("pool", lambda e: e.memset(lrT[32:33, :], 1.0), writes=["lrT"])
    sc.op("pool", lambda e: e.memset(V1[:, :, 128:129], 1.0), writes=["V1ones"])
    sc.op("pool", lambda e: e.memset(mv1[:, :, 128:129], 1.0), writes=["mv1ones"])

    sc.stage = 1
    sc.op("dve", lambda e: e.tensor_tensor(out=junk[:, 0:64], in0=lamp[:, 0:64], in1=lamp[:, 64:128], op=ALU.mult),
          reads=["lamp"], writes=["junk"])
    sc.op("dve", lambda e: e.tensor_reduce(out=lamt[:, 0:1], in_=junk[:, 0:64], axis=mybir.AxisListType.X, op=ALU.add),
          reads=["junk"], writes=["lamt0"])
    sc.op("dve", lambda e: e.tensor_tensor(out=junk[:, 64:128], in0=lamp[:, 128:192], in1=lamp[:, 192:256], op=ALU.mult),
          reads=["lamp"], writes=["junk2"])
    sc.op("dve", lambda e: e.tensor_reduce(out=lamt[:, 1:2], in_=junk[:, 64:128], axis=mybir.AxisListType.X, op=ALU.add),
          reads=["junk2"], writes=["lamt1"])
    sc.op("act", lambda e: e.activation(out=lamt[:, 2:4], in_=lamt[:, 0:2], func=AF.Exp),
          reads=["lamt0", "lamt1"], writes=["lamt2"])
    sc.op("dve", lambda e: e.tensor_tensor(out=lam[:], in0=lamt[:, 2:3], in1=lamt[:, 3:4], op=ALU.subtract),
          reads=["lamt2"], writes=["lam"])
    sc.op("dve", lambda e: e.tensor_scalar(out=lam[:], in0=lam[:], scalar1=float(LAM_INIT), scalar2=None, op0=ALU.add),
          reads=["lam"], writes=["lam"])

    sc.stage = 2
    pk, rk = gbank()
    def mk_mm(e):
        ins = None
        for kc in range(NKC):
            ins = e.matmul(pk[:, 0:256], lhsT=wmb[:, kc, 0:128], rhs=memTb[:, kc, :], start=(kc == 0), stop=(kc == NKC - 1))
        return ins
    sc.op("pe", mk_mm, reads=XB0, writes=[rk])
    sc.op("act", lambda e: e.activation(out=mkT[:], in_=pk[:, 0:256], func=AF.Copy), reads=[rk], writes=["mkT"])
    pv_, rv_ = gbank()
    def mv_mm(e):
        ins = None
        for mb in range(2):
            for kc in range(NKC):
                ins = e.matmul(pv_[:, mb * 128:(mb + 1) * 128], lhsT=memTb[:, kc, mb * 128:(mb + 1) * 128],
                               rhs=wmb[:, kc, 128:256], start=(kc == 0), stop=(kc == NKC - 1))
        return ins
    sc.op("pe", mv_mm, reads=XB0, writes=[rv_])
    sc.op("act", lambda e: e.activation(out=mv1[:, :, 0:128], in_=pv_[:, 0:256].rearrange("p (m d) -> p m d", d=128), func=AF.Copy),
          reads=[rv_], writes=["mv1"])

    xTv = xT.rearrange("(k p) s -> p k s", p=128)
    mixv = mixbuf.ap().rearrange("(n p) c -> p n c", p=128)
    mix_stores = []
    ptc = [0]
    psc = [0]

    def issue_x(t):
        T0 = t * 512
        for j in range(4):
            sc.dma("pool", lambda e, j=j, T0=T0: e.dma_start(out=xb[:, 4 * j:4 * j + 4, :], in_=xTv[:, 4 * j:4 * j + 4, T0:T0 + 512]),
                   writes=[("xb", j)])

    def issue_pos(t):
        T0 = t * 512
        sc.dma("sp", lambda e, T0=T0: e.dma_start(out=posi[:], in_=posr[:, T0:T0 + 512]), writes=["posK"])

    XB = [("xb", j) for j in range(4)]

    def front(t):
        T0 = t * 512
        pb = t % 2
        qT2 = qT2s[pb]
        mix = mixs[pb]
        dgs = dgss[pb]
        sc.stage = 3
        A, K, R, M = T1, bufK, bufR, T2
        sc.op("dve", lambda e: e.tensor_copy(out=A[:], in_=bufK[:]), reads=["posK"], writes=["T1"])
        sc.op("dve", lambda e: e.tensor_scalar(out=A[:], in0=A[:], scalar1=col(CI_INVF), scalar2=None, op0=ALU.mult),
              reads=["T1", "cst"], writes=["T1"])
        sc.op("dve", lambda e: e.tensor_scalar(out=K[:], in0=A[:], scalar1=float(1.0 / TWO_PI), scalar2=None, op0=ALU.mult),
              reads=["T1"], writes=["posK"])
        sc.op("dve", lambda e: e.scalar_tensor_tensor(out=R[:], in0=K[:], scalar=float(-C1), in1=A[:], op0=ALU.mult, op1=ALU.add),
              reads=["posK", "T1"], writes=["bufR"])
        sc.op("dve", lambda e: e.scalar_tensor_tensor(out=R[:], in0=K[:], scalar=float(-C2), in1=R[:], op0=ALU.mult, op1=ALU.add),
              reads=["posK", "bufR"], writes=["bufR"])
        if t + 1 < NT:
            sc.stage = 0
            issue_pos(t + 1)
            sc.stage = 3
        sc.op("dve", lambda e: e.tensor_scalar(out=M[:], in0=R[:], scalar1=float(math.pi), scalar2=float(-TWO_PI), op0=ALU.is_gt, op1=ALU.mult),
              reads=["bufR"], writes=["T2"])
        sc.op("dve", lambda e: e.tensor_tensor(out=sinS[:], in0=M[:], in1=R[:], op=ALU.add),
              reads=["T2", "bufR"], writes=["sinS"])
        sc.op("dve", lambda e: e.tensor_scalar(out=M[:], in0=R[:], scalar1=float(math.pi / 2), scalar2=float(-TWO_PI), op0=ALU.is_gt, op1=ALU.mult),
              reads=["bufR"], writes=["T2"])
        sc.op("dve", lambda e: e.tensor_tensor(out=cosT[:], in0=M[:], in1=R[:], op=ALU.add),
              reads=["T2", "bufR"], writes=["cosT"])
        sc.op("act", lambda e: e.activation(out=sinS[:], in_=sinS[:], func=AF.Sin), reads=["sinS"], writes=["sinS"])
        sc.op("act", lambda e: e.activation(out=cosT[:], in_=cosT[:], func=AF.Sin, bias=col(CI_HALFPI)), reads=["cosT", "cst"], writes=["cosT"])
        sc.op("dve", lambda e: e.tensor_scalar(out=sinS[:], in0=sinS[:], scalar1=col(CI_SGN), scalar2=None, op0=ALU.mult),
              reads=["sinS", "cst"], writes=["sinS"])
        sc.op("dve", lambda e: e.tensor_scalar(out=cosT[:], in0=cosT[:], scalar1=col(CI_NOTROT), scalar2=None, op0=ALU.max),
              reads=["cosT", "cst"], writes=["cosT"])

        sc.stage = 4
        for tb in range(4):
            pa_, ra_ = gbank()
            pb_, rb_ = gbank()
            def tproj(e, tb=tb, pa_=pa_, pb_=pb_):
                ins = None
                for kc in range(NKC):
                    lt = xb[:, kc, tb * 128:(tb + 1) * 128]
                    e.matmul(pa_[:, :], lhsT=lt, rhs=WT[:, kc, 0:512], start=(kc == 0), stop=(kc == NKC - 1))
                    ins = e.matmul(pb_[:, :], lhsT=lt, rhs=WT[:, kc, 512:1024], start=(kc == 0), stop=(kc == NKC - 1))
                return ins
            sc.op("pe", tproj, reads=XB + WTres, writes=[ra_, rb_])
            sc.op("act", lambda e, tb=tb, pa_=pa_: e.activation(out=gsil[:, tb, :], in_=pa_[:, 256:512], func=AF.Silu),
                  reads=[ra_], writes=[("gsil", tb)])
            sc.op("act", lambda e, tb=tb, pb_=pb_: e.activation(
                out=dgs[:, tb, :], in_=pb_[:, 256:384], func=AF.Silu), reads=[rb_], writes=[("dgs", pb, tb)])
            sc.op("act", lambda e, tb=tb, pb_=pb_: e.activation(
                out=mgs[:, tb, :], in_=pb_[:, 384:512], func=AF.Silu), reads=[rb_], writes=[("mgs", tb)])
            sc.op("dve", lambda e, tb=tb, pa_=pa_: e.tensor_copy(out=gvb[:, tb, :], in_=pa_[:, 0:256]),
                  reads=[ra_], writes=[("gvb", tb)])
            sc.op("dve", lambda e, tb=tb, pb_=pb_: e.tensor_copy(out=gktok[:, tb, :], in_=pb_[:, 0:128]),
                  reads=[rb_], writes=[("gktok", tb)])
            sc.op("dve", lambda e, tb=tb, pb_=pb_, t=t: e.tensor_copy(out=V1[:, 4 * t + tb, 0:128], in_=pb_[:, 128:256]),
                  reads=[rb_], writes=[("V1", 4 * t + tb)])
            sc.op("pool", lambda e, tb=tb: e.tensor_tensor(out=gsil[:, tb, :], in0=gsil[:, tb, :], in1=glag[:], op=ALU.mult),
                  reads=[("gsil", tb), "glag"], writes=[("gsil", tb)])
            sc.op("pool", lambda e, tb=tb: e.tensor_tensor(out=dgs[:, tb, :], in0=dgs[:, tb, :], in1=dgg[:], op=ALU.mult),
                  reads=[("dgs", pb, tb), "dgg"], writes=[("dgs", pb, tb)])
            yield

        sc.stage = 5
        def fproj(name, ncols=128):
            p_, r_ = gbank()
            c0 = FB[name]
            def mm(e, p_=p_, c0=c0, ncols=ncols):
                ins = None
                for kc in range(NKC):
                    ins = e.matmul(p_[0:ncols, :], lhsT=WF[:, kc, c0:c0 + ncols], rhs=xb[:, kc, :], start=(kc == 0), stop=(kc == NKC - 1))
                return ins
            sc.op("pe", mm, reads=XB + WFres, writes=[r_])
            return p_, r_

        p_, r_ = fproj("lr", 16)
        sc.op("act", lambda e, p_=p_: e.activation(out=lrT[0:16, :], in_=p_[0:16, :], func=AF.Copy), reads=[r_], writes=["lrT"])
        yield
        for nm, sw in (("dq", "dqs"), ("dk", "dks")):
            p_, r_ = fproj(nm)
            sc.op("dve", lambda e, p_=p_: e.tensor_tensor(out=T1[:], in0=p_[:, :], in1=cosT[:], op=ALU.mult),
                  reads=[r_, "cosT"], writes=["T1"])
            yield
            p2, r2 = fproj(sw)
            sc.op("dve", lambda e, p2=p2: e.tensor_tensor(out=T2[:], in0=p2[:, :], in1=sinS[:], op=ALU.mult),
                  reads=[r2, "sinS"], writes=["T2"])
            if nm == "dq":
                for c in range(2):
                    sc.op("pool", lambda e, c=c: e.tensor_tensor(out=qT2[c * 64:(c + 1) * 64, c, :], in0=T1[c * 64:(c + 1) * 64, :], in1=T2[c * 64:(c + 1) * 64, :], op=ALU.add),
                          reads=["T1", "T2"], writes=[("qT2", pb)])
            else:
                sc.op("pool", lambda e, T0=T0: e.tensor_tensor(out=KT[:, T0:T0 + 512], in0=T1[:], in1=T2[:], op=ALU.add),
                      reads=["T1", "T2"], writes=[("KT", t)])
            yield
        pl, rl = gbank()
        def lg_mm(e, pl=pl):
            ins = None
            for c in range(4):
                ins = e.matmul(pl[:, c * 128:(c + 1) * 128], lhsT=lrT[0:33, c * 128:(c + 1) * 128], rhs=wup_s[0:33, :], start=True, stop=True)
            return ins
        sc.op("pe", lg_mm, reads=["lrT", "wup"], writes=[rl])
        sc.op("act", lambda e, pl=pl: e.activation(out=SP[:], in_=pl[:, :], func=AF.Exp, scale=-1.0), reads=[rl], writes=["SP"])
        sc.op("act", lambda e: e.activation(out=SP[:], in_=SP[:], func=AF.Ln, bias=col(CI_ONE)), reads=["SP", "cst"], writes=["SP"])
        p_, r_ = fproj("mq")
        sc.op("act", lambda e, p_=p_: e.activation(out=mqT[:], in_=p_[:, :], func=AF.Copy), reads=[r_], writes=["mqT"])
        yield
        pbt, rbt = gbank()
        def bt_mm(e, pbt=pbt):
            ins = None
            for c in range(4):
                ins = e.matmul(pbt[:, c * 128:(c + 1) * 128], lhsT=SP[:, c * 128:(c + 1) * 128], rhs=tinc[:], start=True, stop=True)
            return ins
        sc.op("pe", bt_mm, reads=["SP", "tinc"], writes=[rbt])
        sc.op("act", lambda e, pbt=pbt: e.activation(out=EB[:], in_=pbt[:, :], func=AF.Exp), reads=[rbt], writes=["EB"])
        sc.op("act", lambda e, pbt=pbt: e.activation(out=ENB[:], in_=pbt[:, :], func=AF.Exp, scale=-1.0), reads=[rbt], writes=["ENB"])
        pd_, rd_ = gbank()
        def d_mm(e, pd_=pd_):
            ins = None
            for c in range(4):
                ins = e.matmul(pd_[:, c * 128:(c + 1) * 128], lhsT=texc[:], rhs=SP[:, c * 128:(c + 1) * 128], start=True, stop=True)
            return ins
        sc.op("pe", d_mm, reads=["SP", "texc"], writes=[rd_])
        sc.op("act", lambda e, pd_=pd_: e.activation(out=SP[:], in_=pd_[:, :], func=AF.Exp), reads=[rd_], writes=["SP"])
        sc.op("pool", lambda e: e.tensor_tensor(out=kdec[:].rearrange("p a b -> p (a b)"), in0=gktok[:].rearrange("p a b -> p (a b)"), in1=SP[:], op=ALU.mult),
              reads=[("gktok", i) for i in range(4)] + ["SP"], writes=["kdec"])
        yield
        p_, r_ = fproj("gq")
        sc.op("dve", lambda e, p_=p_: e.scalar_tensor_tensor(out=gqt[:], in0=p_[:, :], scalar=float(128 ** -0.5), in1=EB[:], op0=ALU.mult, op1=ALU.mult),
              reads=[r_, "EB"], writes=["gqt"])
        yield
        p_, r_ = fproj("gk")
        sc.op("dve", lambda e, p_=p_: e.tensor_tensor(out=gkt[:], in0=p_[:, :], in1=ENB[:], op=ALU.mult),
              reads=[r_, "ENB"], writes=["gkt"])
        sc.stage = 0
        if t + 1 < NT:
            issue_x(t + 1)
        yield

        sc.stage = 7
        for mb in range(2):
            pq_, rq_ = gbank()
            sc.op("pe", lambda e, pq_=pq_, mb=mb: e.matmul(pq_[:, :], lhsT=mkT[:, mb * 128:(mb + 1) * 128], rhs=mqT[:], start=True, stop=True),
                  reads=["mkT", "mqT"], writes=[rq_])
            sc.op("act", lambda e, pq_=pq_, mb=mb: e.activation(out=pm[:, mb, :], in_=pq_[:, :], func=AF.Exp, scale=float(128 ** -0.5)),
                  reads=[rq_], writes=[("pm", mb)])
        yield
        for half in range(2):
            po_, ro_ = gbank()
            def mo_mm(e, po_=po_, half=half):
                ins = None
                for i in range(2):
                    tb = half * 2 + i
                    for mb in range(2):
                        ins = e.matmul(po_[:, i * 144:i * 144 + 129], lhsT=pm[:, mb, tb * 128:(tb + 1) * 128], rhs=mv1[:, mb, :], start=(mb == 0), stop=(mb == 1))
                return ins
            sc.op("pe", mo_mm, reads=[("pm", 0), ("pm", 1), "mv1", "mv1ones"], writes=[ro_])
            for i in range(2):
                tb = half * 2 + i
                sc.op("dve", lambda e, po_=po_, i=i: e.reciprocal(out=sml[:, 4:5], in_=po_[:, i * 144 + 128:i * 144 + 129]),
                      reads=[ro_], writes=["sml4"])
                sc.op("dve", lambda e, po_=po_, i=i, tb=tb: e.scalar_tensor_tensor(out=mix[:, tb, 384:512], in0=po_[:, i * 144:i * 144 + 128], scalar=sml[:, 4:5], in1=mgs[:, tb, :], op0=ALU.mult, op1=ALU.mult),
                      reads=[ro_, "sml4", ("mgs", tb)], writes=[("mix", pb, tb, "m")])
            yield

        sc.stage = 6
        for c in range(4):
            cs = slice(c * 128, (c + 1) * 128)
            px, rx = gbank()
            py, ry = gbank()
            sc.op("pe", lambda e, px=px, cs=cs: e.matmul(px[:, 256:384], lhsT=gkt[:, cs], rhs=gqt[:, cs], start=True, stop=True),
                  reads=["gkt", "gqt"], writes=[rx])
            sc.op("dve", lambda e, px=px: e.tensor_tensor(out=sTm[:], in0=px[:, 256:384], in1=tri[:], op=ALU.mult),
                  reads=[rx, "tri"], writes=["sTm"])
            sc.op("pe", lambda e, py=py, c=c: e.matmul(py[:, 0:256], lhsT=kdec[:, c, :], rhs=gvb[:, c, :], start=True, stop=True),
                  reads=["kdec", ("gvb", c)], writes=[ry])
            yield
            sc.op("pe", lambda e, px=px, cs=cs: e.matmul(px[:, 0:256], lhsT=gqt[:, cs], rhs=Sbf[:], start=True, stop=False),
                  reads=["gqt", "Sbf"], writes=[rx])
            sc.op("pe", lambda e, px=px, c=c: e.matmul(px[:, 0:256], lhsT=sTm[:], rhs=gvb[:, c, :], start=False, stop=True),
                  reads=["sTm", ("gvb", c)], writes=[rx])
            sc.op("dve", lambda e, py=py, c=c: e.scalar_tensor_tensor(out=Sst[:], in0=Sst[:], scalar=EB[:, c * 128 + 127:c * 128 + 128], in1=py[:, 0:256], op0=ALU.mult, op1=ALU.add),
                  reads=["S", "EB", ry], writes=["S"])
            sc.op("dve", lambda e: e.tensor_copy(out=Sbf[:], in_=Sst[:]), reads=["S"], writes=["Sbf"])
            sc.op("act", lambda e, px=px: e.activation(out=junk[:], in_=px[:, 0:256], func=AF.Square, accum_out=sml[:, 0:1]),
                  reads=[rx], writes=["junk", "sml0"])
            sc.op("act", lambda e: e.activation(out=sml[:, 1:2], in_=sml[:, 0:1], func=AF.Ln, scale=float(1.0 / 256), bias=col(CI_EPSG)),
                  reads=["sml0", "cst"], writes=["sml1"])
            sc.op("act", lambda e: e.activation(out=sml[:, 2:3], in_=sml[:, 1:2], func=AF.Exp, scale=-0.5),
                  reads=["sml1"], writes=["sml2"])
            sc.op("dve", lambda e, px=px, c=c: e.scalar_tensor_tensor(out=mix[:, c, 0:256], in0=px[:, 0:256], scalar=sml[:, 2:3], in1=gsil[:, c, :], op0=ALU.mult, op1=ALU.mult),
                  reads=[rx, "sml2", ("gsil", c)], writes=[("mix", pb, c, "g")])
            yield

    def diff_steps(t):
        sc.stage = 8
        pb = t % 2
        qT2 = qT2s[pb]
        started = set()

        def issue_s(hq, kb):
            d = kb - (4 * t + 2 * hq)
            lo = 128 if d == 1 else 0
            n = 256 - lo
            ps = PS[psc[0] % 2]
            rps = ("PS", psc[0] % 2)
            psc[0] += 1
            pt = pT[ptc[0] % NPT]
            rpt = ("pT", ptc[0] % NPT)
            ptc[0] += 1
            psv = ps[:, :]
            ptv = pt[:, :]
            q0 = hq * 256
            sc.op("pe", lambda e, psv=psv, kb=kb, q0=q0: e.matmul(
                psv, lhsT=KT[:, kb * 128:(kb + 1) * 128], rhs=qT2[:, :, q0:q0 + 256], start=True, stop=True),
                reads=[("KT", kb // 4), ("qT2", pb)], writes=[rps])
            sc.op("act", lambda e, psv=psv, ptv=ptv: e.activation(out=ptv, in_=psv, func=AF.Exp, scale=0.125),
                  reads=[rps], writes=[rpt])
            if d >= 0:
                for c in range(2):
                    sc.op("pool", lambda e, pt=pt, c=c, lo=lo: e.tensor_tensor(
                        out=pt[:, c * 256 + lo:c * 256 + lo + 128], in0=pt[:, c * 256 + lo:c * 256 + lo + 128], in1=tri[:], op=ALU.mult),
                        reads=[rpt, "tri"], writes=[rpt])
            return pt, rpt, lo

        def issue_pv(hq, kb, pt, rpt, lo):
            wr = set()
            plan = []
            for c in range(2):
                for ql in range(lo // 128, 2):
                    qb = 2 * hq + ql
                    reg, rk_ = pa_region(c, qb)
                    first = rk_ not in started
                    started.add(rk_)
                    wr.add(rk_)
                    plan.append((reg, c * 256 + ql * 128, first, kb == 4 * t + qb))
            def pv(e, plan=plan, pt=pt, kb=kb):
                ins = None
                for reg, off_, first, last in plan:
                    ins = e.matmul(reg, lhsT=pt[:, off_:off_ + 128], rhs=V1[:, kb, :], start=first, stop=last, skip_group_check=True)
                return ins
            sc.op("pe", pv, reads=[rpt, ("V1", kb), "V1ones"], writes=sorted(wr))

        steps = [(hq, kb) for hq in range(2) for kb in range(4 * t + 2 * hq + 2)]
        pend = issue_s(*steps[0])
        for si in range(len(steps)):
            nxt = issue_s(*steps[si + 1]) if si + 1 < len(steps) else None
            issue_pv(steps[si][0], steps[si][1], *pend)
            pend = nxt
            yield

    def epilogue(t):
        sc.stage = 8
        pb = t % 2
        mix = mixs[pb]
        dgs = dgss[pb]
        for qb in range(4):
            r1, rr1 = pa_region(0, qb)
            r2, rr2 = pa_region(1, qb)
            sc.op("dve", lambda e, r1=r1: e.reciprocal(out=sml[:, 6:7], in_=r1[:, 128:129]), reads=[rr1], writes=["sml6"])
            sc.op("dve", lambda e, r2=r2: e.reciprocal(out=sml[:, 7:8], in_=r2[:, 128:129]), reads=[rr2], writes=["sml7"])
            sc.op("dve", lambda e: e.tensor_tensor(out=sml[:, 8:9], in0=sml[:, 7:8], in1=lam[:], op=ALU.mult), reads=["sml7", "lam"], writes=["sml8"])
            sc.op("dve", lambda e, r2=r2: e.tensor_scalar(out=tmpd[:], in0=r2[:, 0:128], scalar1=sml[:, 8:9], scalar2=None, op0=ALU.mult),
                  reads=[rr2, "sml8"], writes=["tmpd"])
            sc.op("dve", lambda e, r1=r1: e.scalar_tensor_tensor(out=dif[:], in0=r1[:, 0:128], scalar=sml[:, 6:7], in1=tmpd[:], op0=ALU.mult, op1=ALU.subtract),
                  reads=[rr1, "sml6", "tmpd"], writes=["dif"])
            sc.op("act", lambda e: e.activation(out=junk2[:], in_=dif[:], func=AF.Square, accum_out=sml[:, 9:10]),
                  reads=["dif"], writes=["junk2", "sml9"])
            sc.op("act", lambda e: e.activation(out=sml[:, 10:11], in_=sml[:, 9:10], func=AF.Ln, scale=float(1.0 / 128), bias=col(CI_EPSD)),
                  reads=["sml9", "cst"], writes=["sml10"])
            sc.op("act", lambda e: e.activation(out=sml[:, 11:12], in_=sml[:, 10:11], func=AF.Exp, scale=-0.5),
                  reads=["sml10"], writes=["sml11"])
            sc.op("dve", lambda e, qb=qb: e.scalar_tensor_tensor(out=mix[:, qb, 256:384], in0=dif[:], scalar=sml[:, 11:12], in1=dgs[:, qb, :], op0=ALU.mult, op1=ALU.mult),
                  reads=["dif", "sml11", ("dgs", pb, qb)], writes=[("mix", pb, qb, "d")])
        sc.stage = 9
        mixres = [("mix", pb, tb, k) for tb in range(4) for k in ("g", "d", "m")]
        mix_stores.append(sc.dma("sp", lambda e, t=t: e.dma_start(out=mixv[:, 4 * t:4 * t + 4, :], in_=mix[:]), reads=mixres))

    sc.stage = 0
    issue_x(0)
    issue_pos(0)
    for _ in front(0):
        pass
    for t in range(NT):
        if t > 0:
            for _ in front(t):
                pass
        for _ in diff_steps(t):
            pass
        epilogue(t)

    sc.stage = 10
    cc = sc.collective(lambda e: e.collective_compute(
        "AllGather", mybir.AluOpType.bypass, replica_groups=[list(range(8))],
        ins=[mixbuf.ap().opt()], outs=[gath.ap().opt()]), reads=[], writes=["gath"])
    cc.waits.extend(o for o in mix_stores if o.token is not None)

    sc.stage = 11
    allkv = [("KT", i) for i in range(NT)] + [("V1", i) for i in range(S // 128)] + ["V1ones"]
    woutv = wout.rearrange("(k p) n -> p k n", p=128)
    for j in range(4):
        sc.dma("pool", lambda e, j=j: e.dma_start(out=WOUT[:, 4 * j:4 * j + 4, :], in_=woutv[:, 4 * j:4 * j + 4, :]),
               writes=[("WOUT", j)] + (allkv if j == 0 else []))
    WOres = [("WOUT", j) for j in range(4)]
    a2 = WTres + WFres
    sc.dma("sp", lambda e: e.dma_start(out=lng, in_=lng_d), writes=["lng"] + a2)
    sc.dma("sp", lambda e: e.dma_start(out=lnb, in_=lnb_d), writes=["lnb"])
    pidc = {}

    def pid_of(e, name):
        if name not in pidc:
            pidc[name] = e.partition_id()
        return pidc[name]

    gv = gath.ap()
    sel = nc.dram_tensor("sel", [4 * SQ, 512], BF16)
    selv = sel.ap().rearrange("(h s) c -> s h c", h=4)
    for h in range(4):
        def ldsel(e, h=h):
            pid = e.partition_id()
            row = ((pid // 4) * 4 + h) * S + (pid % 4) * SQ
            return e.dma_start(out=sel.ap()[h * SQ:(h + 1) * SQ, :], in_=gv[bass.ds(row, SQ), :])
        sc.dma("sp", ldsel, reads=["gath"], writes=[("sel", h)])
    alpha = float((2.0 * 1) ** 0.25)
    out_stores = []
    for tb in range(SQ // 128):
        bi = tb % 2
        sc.stage = 11.1
        sc.dma("pool", lambda e, tb=tb, bi=bi: e.dma_start(out=mrow[bi], in_=selv[tb * 128:(tb + 1) * 128, :, :]),
               reads=[("sel", h) for h in range(4)], writes=[("mrow", bi, h) for h in range(4)])
        sc.dma("sp", lambda e, tb=tb, bi=bi: e.dma_start(out=xrow[bi], in_=xq[tb * 128:(tb + 1) * 128, :]), writes=[("xrow", bi)] + [("yrow", bi, cb) for cb in range(4)])
        sc.stage = 11.2
        for h in range(4):
            ptb, rtb = gbank()
            def tr(e, h=h, bi=bi, ptb=ptb):
                ins = None
                for j in range(4):
                    ins = e.transpose(ptb[:, j * 128:(j + 1) * 128], mrow[bi][:, h, j * 128:(j + 1) * 128], ident[:])
                return ins
            sc.op("pe", tr, reads=[("mrow", bi, hh) for hh in range(4)] + ["ident"], writes=[rtb])
            sc.op("act", lambda e, h=h, bi=bi, ptb=ptb: e.activation(
                out=mixT[bi][:, h * 4:h * 4 + 4, :], in_=ptb[:, :].rearrange("p (k c) -> p k c", c=128), func=AF.Copy),
                reads=[rtb], writes=[("mixT", bi, h)])
        sc.stage = 11.3
        for cb in range(4):
            po_, ro_ = gbank()
            def omm(e, po_=po_, cb=cb, bi=bi):
                ins = None
                for fc in range(16):
                    ins = e.matmul(po_[:, :], lhsT=mixT[bi][:, fc, :], rhs=WOUT[:, fc, cb * 512:(cb + 1) * 512], start=(fc == 0), stop=(fc == 15))
                return ins
            sc.op("pe", omm, reads=[("mixT", bi, hh) for hh in range(4)] + WOres, writes=[ro_])
            sc.op("dve", lambda e, po_=po_, cb=cb, bi=bi: e.scalar_tensor_tensor(
                out=yrow[bi][:, cb * 512:(cb + 1) * 512], in0=xrow[bi][:, cb * 512:(cb + 1) * 512], scalar=alpha, in1=po_[:, :], op0=ALU.mult, op1=ALU.add),
                reads=[ro_, ("xrow", bi)], writes=[("yrow", bi, cb), ("xrow", bi)])
            sc.op("dve", lambda e, cb=cb, bi=bi: e.bn_stats(out=bnst[:, cb * 6:cb * 6 + 6], in_=yrow[bi][:, cb * 512:(cb + 1) * 512]),
                  reads=[("yrow", bi, cb)], writes=[("bnst", cb)])
        sc.stage = 11.4
        yres = [("yrow", bi, cb) for cb in range(4)]
        sc.op("dve", lambda e: e.bn_aggr(out=bnst[:, 24:26], in_=bnst[:, 0:24]), reads=[("bnst", cb) for cb in range(4)], writes=["mv"])
        sc.op("act", lambda e: e.activation(out=bnst[:, 26:27], in_=bnst[:, 25:26], func=AF.Ln, bias=col(CI_EPSL)), reads=["mv", "cst"], writes=["lnv"])
        sc.op("act", lambda e: e.activation(out=bnst[:, 27:28], in_=bnst[:, 26:27], func=AF.Exp, scale=-0.5), reads=["lnv"], writes=["rstd"])
        sc.op("dve", lambda e, bi=bi: e.scalar_tensor_tensor(out=yrow[bi], in0=yrow[bi], scalar=bnst[:, 24:25], in1=lng, op0=ALU.subtract, op1=ALU.mult),
              reads=yres + ["mv", "lng"], writes=yres)
        sc.op("dve", lambda e, bi=bi: e.scalar_tensor_tensor(out=yrow[bi], in0=yrow[bi], scalar=bnst[:, 27:28], in1=lnb, op0=ALU.mult, op1=ALU.add),
              reads=yres + ["rstd", "lnb"], writes=yres)
        out_stores.append(sc.dma("sp", lambda e, tb=tb, bi=bi: e.dma_start(out=y[tb * 128:(tb + 1) * 128, :], in_=yrow[bi]), reads=yres))

    sc.fence("sp", [o for o in out_stores if o.token is not None])
    with nc.allow_low_precision("bf16 matmul operands, fp32 accumulation"):
        sc.emit(nc, es)
    es.close()
    return nc


def _prep_inputs(x, mem, positions, w_in, w_gk_up, b_gk_up, gla_norm_g, lambda_q1, lambda_k1,
                 lambda_q2, lambda_k2, diff_norm_g, w_mem_kv, w_out, ln_g, ln_b):
    f32 = np.float32
    x = np.asarray(x, f32)
    mem = np.asarray(mem, f32)
    positions = np.asarray(positions, np.int32)
    w_in = np.asarray(w_in, f32)[0]
    w_up = np.asarray(w_gk_up, f32)[0]
    b_up = np.asarray(b_gk_up, f32)[0]
    w_mem = np.asarray(w_mem_kv, f32)[0]
    w_o = np.asarray(w_out, f32)[0]

    def rep(v, n=128):
        return np.ascontiguousarray(np.broadcast_to(np.asarray(v, f32).reshape(1, -1), (n, np.asarray(v).size)))

    half = (np.arange(0, 16, 2, dtype=f32) / f32(16)).astype(f32)
    inv_freq = np.power(f32(500000.0), -half).astype(f32)
    cst = np.zeros((128, 8), f32)
    for p in range(128):
        d = p % 64
        if d < 16:
            cst[p, CI_INVF] = inv_freq[d % 8]
            cst[p, CI_SGN] = -1.0 if d < 8 else 1.0
            cst[p, CI_NOTROT] = -2.0
        else:
            cst[p, CI_NOTROT] = 1.0
    cst[:, CI_HALFPI] = math.pi / 2
    cst[:, CI_ONE] = 1.0
    cst[:, CI_EPSG] = 1e-6
    cst[:, CI_EPSD] = 1e-5
    cst[:, CI_EPSL] = 1e-5
    jj, ii = np.meshgrid(np.arange(128), np.arange(128), indexing="ij")
    tri = (jj <= ii).astype(f32)
    tinc = tri * f32(-1.0 / 16)
    texc = (jj > ii).astype(f32) * f32(-1.0 / 16)
    ident = np.eye(128, dtype=f32)
    swap = np.arange(128)
    for p in range(128):
        d = p % 64
        if d < 8:
            swap[p] = p + 8
        elif d < 16:
            swap[p] = p - 8

    rows = []
    for h in range(4):
        rows.append(np.arange(h * 256, h * 256 + 256))
        rows.append(np.arange(1024 + h * 128, 1024 + h * 128 + 128))
        rows.append(np.arange(1536 + h * 128, 1536 + h * 128 + 128))
    woutP = np.ascontiguousarray(w_o[np.concatenate(rows)])
    lamp = rep(np.concatenate([np.asarray(a, f32).reshape(-1) for a in (lambda_q1, lambda_k1, lambda_q2, lambda_k2)]))
    glag = rep(np.asarray(gla_norm_g, f32).reshape(-1))
    dgg = rep(np.asarray(diff_norm_g, f32).reshape(-1))
    lng = rep(np.asarray(ln_g, f32).reshape(-1))
    lnb = rep(np.asarray(ln_b, f32).reshape(-1))

    xTs = [np.ascontiguousarray(x[b].T) for b in range(2)]
    memTs = [np.ascontiguousarray(mem[b].T) for b in range(2)]
    posrs = [np.ascontiguousarray(np.broadcast_to(positions[b][None, :], (128, S))) for b in range(2)]
    in_maps = []
    for c in range(8):
        b, h = c // 4, c % 4
        gq = w_in[:, 0 + h * 128:0 + h * 128 + 128]
        gk = w_in[:, 512 + h * 128:512 + h * 128 + 128]
        gv = w_in[:, 1024 + h * 256:1024 + h * 256 + 256]
        gg = w_in[:, 2048 + h * 256:2048 + h * 256 + 256]
        lr = w_in[:, 3072:3088]
        dq = w_in[:, 3088 + h * 128:3088 + h * 128 + 128]
        dk = w_in[:, 3600 + h * 128:3600 + h * 128 + 128]
        dv = w_in[:, 4112 + h * 128:4112 + h * 128 + 128]
        dg = w_in[:, 4624 + h * 128:4624 + h * 128 + 128]
        mq = w_in[:, 5136 + h * 128:5136 + h * 128 + 128]
        mg = w_in[:, 5648 + h * 128:5648 + h * 128 + 128]
        wF = np.ascontiguousarray(np.concatenate([gq, gk, dq, dk, dq[:, swap], dk[:, swap], mq, lr], axis=1))
        wT = np.ascontiguousarray(np.concatenate([gv, gg, gk, dv, dg, mg], axis=1))
        wm = np.ascontiguousarray(np.concatenate([w_mem[:, h * 128:h * 128 + 128], w_mem[:, 512 + h * 128:512 + h * 128 + 128]], axis=1))
        wup = np.zeros((33, 128), f32)
        wup[0:16] = w_up[:, h * 128:h * 128 + 128]
        wup[32] = b_up[h * 128:h * 128 + 128]
        in_maps.append({
            "xT": xTs[b], "xq": np.ascontiguousarray(x[b, h * SQ:(h + 1) * SQ, :]),
            "wF": wF, "wT": wT, "memT": memTs[b], "wm": wm, "wup": wup, "wout": woutP,
            "posr": posrs[b], "glag": glag, "dgg": dgg, "lamp": lamp, "lng": lng, "lnb": lnb,
            "cst": cst, "tri": tri, "tinc": tinc, "texc": texc, "ident": ident,
        })
    return in_maps


_NC_CACHE = {}


def _set_S(s_):
    global S, NT, SQ
    S = s_
    NT = S // 512
    SQ = S // 4


def kernel(**inputs):
    in_maps = _prep_inputs(**inputs)
    if "nc" not in _NC_CACHE:
        _NC_CACHE["nc"] = build_nc()
    nc = _NC_CACHE["nc"]
    res = run_bass_kernel_spmd(nc, in_maps, core_ids=list(range(8)))
    out = np.empty((2, S, D), np.float32)
    for c in range(8):
        b, h = c // 4, c % 4
        out[b, h * SQ:(h + 1) * SQ, :] = res.results[c]["y"]
    return out
```
